# Optimizing a Trainium2 kernel written in Bass

```python
import math
import jax, jax.numpy as jnp
from jax import lax
import numpy as np

D_MODEL = 1024
BATCH = 16
SEQ = 2048
DEPTH = 1
DEC_BATCH = 8
DEC_SEQ = 2048
PAST_LEN = 128

N_HEADS = 8
HEAD_DIM = 64
V_HEAD_DIM = 2 * HEAD_DIM
QK_WIDTH = 2 * N_HEADS * HEAD_DIM
V_WIDTH = N_HEADS * V_HEAD_DIM
N_FOURIER_GROUPS = 4
FOURIER_GROUP = 128
FOURIER_WIDTH = N_FOURIER_GROUPS * FOURIER_GROUP
IN_WIDTH = 2 * QK_WIDTH + V_WIDTH + FOURIER_WIDTH
N_BRANCHES = 2
D_FF = 4 * D_MODEL
ROPE_THETA = 10000.0
Q_BLOCK = 128
EPS = 1e-6
LAMBDA_STD = 0.1

kernel_name = "gated_diffattn_fnet_encoder"


def rmsnorm(x, g):
    xf = x.astype(jnp.float32)
    y = xf * lax.rsqrt(jnp.mean(xf * xf, axis=-1, keepdims=True) + EPS)
    return (y * g.astype(jnp.float32)).astype(x.dtype)


def rope(x):
    S, Dh = x.shape[2], x.shape[3]
    half = Dh // 2
    freqs = ROPE_THETA ** (-jnp.arange(half, dtype=jnp.float32) * 2.0 / Dh)
    ang = jnp.arange(S, dtype=jnp.float32)[:, None] * freqs[None, :]
    cos, sin = jnp.cos(ang), jnp.sin(ang)
    xf = x.astype(jnp.float32)
    x1, x2 = xf[..., :half], xf[..., half:]
    out = jnp.concatenate([x1 * cos - x2 * sin, x2 * cos + x1 * sin], axis=-1)
    return out.astype(x.dtype)


def diff_attention(q, k, v, lam):
    B, H2, S, Dh = q.shape
    nb = S // Q_BLOCK
    qb = q.reshape(B, H2, nb, Q_BLOCK, Dh).transpose(2, 0, 1, 3, 4)
    scale = Dh ** -0.5

    def block(qblk):
        s = jnp.einsum('bhqd,bhkd->bhqk', qblk, k, preferred_element_type=jnp.float32) * scale
        p = jax.nn.softmax(s, axis=-1).reshape(B, N_HEADS, 2, Q_BLOCK, S)
        a = p[:, :, 0] - lam * p[:, :, 1]
        return jnp.einsum('bhqk,bhkd->bhqd', a.astype(v.dtype), v)

    o = lax.map(block, qb)
    return o.transpose(1, 2, 0, 3, 4).reshape(B, N_HEADS, S, V_HEAD_DIM)


def encoder_layer(x, lambda_init, g_mix, w_in, g_q, g_k, lam_q1, lam_k1, lam_q2, lam_k2,
                  g_sub, w_attn_br, w_four_br, w_gate, b_gate, w_out, g_mlp, w_up, w_down):
    B, S, _ = x.shape
    h = rmsnorm(x, g_mix)
    proj = h @ w_in
    q, k, v, f = jnp.split(proj, [QK_WIDTH, 2 * QK_WIDTH, 2 * QK_WIDTH + V_WIDTH], axis=-1)

    q = rmsnorm(q.reshape(B, S, 2 * N_HEADS, HEAD_DIM), g_q).transpose(0, 2, 1, 3)
    k = rmsnorm(k.reshape(B, S, 2 * N_HEADS, HEAD_DIM), g_k).transpose(0, 2, 1, 3)
    q, k = rope(q), rope(k)
    v = v.reshape(B, S, N_HEADS, V_HEAD_DIM).transpose(0, 2, 1, 3)
    lam = (jnp.exp(jnp.sum(lam_q1.astype(jnp.float32) * lam_k1.astype(jnp.float32)))
           - jnp.exp(jnp.sum(lam_q2.astype(jnp.float32) * lam_k2.astype(jnp.float32)))
           + lambda_init)
    o = diff_attention(q, k, v, lam)
    o = rmsnorm(o, g_sub) * (1.0 - lambda_init)
    o = o.transpose(0, 2, 1, 3).reshape(B, S, V_WIDTH)
    attn_out = o @ w_attn_br

    fg = f.reshape(B, S, N_FOURIER_GROUPS, FOURIER_GROUP).astype(jnp.float32)
    fr = jnp.real(jnp.fft.fft2(fg, axes=(1, 3), norm='ortho')).astype(x.dtype)
    four_out = fr.reshape(B, S, FOURIER_WIDTH) @ w_four_br

    gates = jax.nn.sigmoid((h @ w_gate + b_gate).astype(jnp.float32)).astype(x.dtype)
    gates = gates.reshape(B, S, N_BRANCHES, D_MODEL)
    mixed = gates[:, :, 0] * attn_out + gates[:, :, 1] * four_out
    x = x + mixed @ w_out

    h2 = rmsnorm(x, g_mlp)
    u = jnp.square(jax.nn.relu(h2 @ w_up))
    return x + u @ w_down


def setup_inputs(seed: int = 0) -> dict:
    key = jax.random.key(seed)
    ks = jax.random.split(key, 20)
    f32 = jnp.float32

    def nrm(k, shape, scale):
        return jax.random.normal(k, shape, f32) * scale

    def gain(k, shape):
        return 1.0 + 0.02 * jax.random.normal(k, shape, f32)

    L = DEPTH
    return {
        "x_prompt": jax.random.normal(ks[0], (BATCH, SEQ, D_MODEL), f32),
        "x_sample": jax.random.normal(ks[1], (DEC_BATCH, DEC_SEQ, D_MODEL), f32),
        "g_mix": gain(ks[2], (L, D_MODEL)),
        "w_in": nrm(ks[3], (L, D_MODEL, IN_WIDTH), D_MODEL ** -0.5),
        "g_q": gain(ks[4], (L, HEAD_DIM)),
        "g_k": gain(ks[5], (L, HEAD_DIM)),
        "lam_q1": nrm(ks[6], (L, HEAD_DIM), LAMBDA_STD),
        "lam_k1": nrm(ks[7], (L, HEAD_DIM), LAMBDA_STD),
        "lam_q2": nrm(ks[8], (L, HEAD_DIM), LAMBDA_STD),
        "lam_k2": nrm(ks[9], (L, HEAD_DIM), LAMBDA_STD),
        "g_sub": gain(ks[10], (L, V_HEAD_DIM)),
        "w_attn_br": nrm(ks[11], (L, V_WIDTH, D_MODEL), V_WIDTH ** -0.5),
        "w_four_br": nrm(ks[12], (L, FOURIER_WIDTH, D_MODEL), FOURIER_WIDTH ** -0.5),
        "w_gate": nrm(ks[13], (L, D_MODEL, N_BRANCHES * D_MODEL), D_MODEL ** -0.5),
        "b_gate": nrm(ks[14], (L, N_BRANCHES * D_MODEL), 0.02),
        "w_out": nrm(ks[15], (L, D_MODEL, D_MODEL), D_MODEL ** -0.5),
        "g_mlp": gain(ks[16], (L, D_MODEL)),
        "w_up": nrm(ks[17], (L, D_MODEL, D_FF), D_MODEL ** -0.5),
        "w_down": nrm(ks[18], (L, D_FF, D_MODEL), D_FF ** -0.5),
    }


def reference(x_prompt, x_sample, g_mix, w_in, g_q, g_k, lam_q1, lam_k1, lam_q2, lam_k2,
              g_sub, w_attn_br, w_four_br, w_gate, b_gate, w_out, g_mlp, w_up, w_down):
    yp, ys = x_prompt, x_sample
    for l in range(DEPTH):
        lambda_init = 0.8 - 0.6 * math.exp(-0.3 * l)
        params = (g_mix[l], w_in[l], g_q[l], g_k[l], lam_q1[l], lam_k1[l], lam_q2[l], lam_k2[l],
                  g_sub[l], w_attn_br[l], w_four_br[l], w_gate[l], b_gate[l], w_out[l],
                  g_mlp[l], w_up[l], w_down[l])
        yp = encoder_layer(yp, lambda_init, *params)
        ys = encoder_layer(ys, lambda_init, *params)
    return (yp, ys)
```

```python
import math
from contextlib import ExitStack

import numpy as np
import ml_dtypes

import concourse.bass as bass
import concourse.mybir as mybir
from concourse.bass_utils import run_bass_kernel_spmd

F32 = mybir.dt.float32
BF16 = mybir.dt.bfloat16
AF = mybir.ActivationFunctionType
ALU = mybir.AluOpType
AX = mybir.AxisListType

N_CORES = 8
SEQ = 2048
DM = 1024
NSEQ_CORE = 3
NT = SEQ // 128
NTC = SEQ // 512
EPS = 1e-6
LAMBDA_INIT = 0.8 - 0.6 * math.exp(-0.3 * 0)

ENGS = ("pe", "act", "dve", "pool", "sp")
EMBED_WAIT = True


class Instr:
    __slots__ = ("eng", "fn", "deps", "is_dma", "sem", "target", "signal")

    def __init__(self, eng, fn, is_dma=False):
        self.eng = eng
        self.fn = fn
        self.deps = []
        self.is_dma = is_dma
        self.sem = None
        self.target = 0
        self.signal = False


class Prog:
    def __init__(self, nc, stack):
        self.nc = nc
        self.stack = stack
        self.streams = {e: [] for e in ENGS}
        self.writer = {}
        self.readers = {}
        self.eng_sem = {}
        for e in ("pe", "act", "dve", "pool"):
            self.eng_sem[e] = stack.enter_context(nc.semaphore("ms_" + e))
        self.dma_sems = {}

    def _track(self, ins, reads, writes):
        deps = ins.deps
        for r in reads:
            w = self.writer.get(r)
            if w is not None:
                deps.append((w, 0))
            self.readers.setdefault(r, []).append(ins)
        for r in writes:
            w = self.writer.get(r)
            if w is not None and w is not ins:
                deps.append((w, 1))
            for rd in self.readers.get(r, ()):
                if rd is not ins:
                    deps.append((rd, 2))
            self.writer[r] = ins
            self.readers[r] = []

    def op(self, eng, fn, reads=(), writes=()):
        ins = Instr(eng, fn)
        self._track(ins, reads, writes)
        self.streams[eng].append(ins)
        return ins

    def dma(self, eng, fn, slot, reads=(), writes=()):
        ins = Instr(eng, fn, is_dma=True)
        if slot not in self.dma_sems:
            sem = self.stack.enter_context(self.nc.semaphore("dq%d" % len(self.dma_sems)))
            self.dma_sems[slot] = [sem, 0]
        ent = self.dma_sems[slot]
        ent[1] += 16
        ins.sem, ins.target = ent[0], ent[1]
        self._track(ins, reads, writes)
        self.streams[eng].append(ins)
        return ins

    @staticmethod
    def _needed(ins, dep, kind):
        if dep.is_dma:
            return True
        if dep.eng == ins.eng and not ins.is_dma:
            return ins.eng != "pe"
        return True

    def emit(self):
        nc = self.nc
        for e in ENGS:
            for ins in self.streams[e]:
                for dep, kind in ins.deps:
                    if not dep.is_dma and self._needed(ins, dep, kind):
                        dep.signal = True
        for e in ENGS:
            cnt = 0
            for ins in self.streams[e]:
                if not ins.is_dma and ins.signal:
                    cnt += 1
                    ins.sem, ins.target = self.eng_sem[e], cnt
        handles = {"pe": "tensor", "act": "scalar", "dve": "vector", "pool": "gpsimd", "sp": "sync"}
        with nc.Block() as block:
            for e in ENGS:
                stream = self.streams[e]

                def body(engine, stream=stream):
                    waited = {}
                    for ins in stream:
                        need = {}
                        for dep, kind in ins.deps:
                            if not self._needed(ins, dep, kind):
                                continue
                            key = id(dep.sem)
                            if dep.target > need.get(key, (None, 0))[1]:
                                need[key] = (dep.sem, dep.target)
                        todo = [(sem, tgt, key) for key, (sem, tgt) in need.items() if waited.get(key, 0) < tgt]
                        emb = None
                        if todo and EMBED_WAIT:
                            emb = todo.pop()
                        for sem, tgt, key in todo:
                            engine.wait_ge(sem, tgt)
                            waited[key] = tgt
                        r = ins.fn(engine)
                        if emb is not None:
                            if r is None:
                                engine.wait_ge(emb[0], emb[1])
                            else:
                                r._wait_ge(emb[0], emb[1])
                            waited[emb[2]] = emb[1]
                        if ins.is_dma:
                            r.then_inc(ins.sem, 16)
                        elif ins.signal:
                            r.then_inc(ins.sem, 1)

                getattr(block, handles[e])(body)


OVN = 41984
CELL = 512


def ovc(lo, n):
    return [("ov", c) for c in range(lo // CELL, (lo + n + CELL - 1) // CELL)]


def build_program(nseq, debug=False):
    nc = bass.Bass("TRN2", target_bir_lowering=False)

    def din(name, shape, dt):
        return nc.dram_tensor(name, list(shape), dt, kind="ExternalInput").ap()

    def dscr(name, shape, dt):
        return nc.dram_tensor(name, list(shape), dt, kind="Internal").ap()

    x_d = din("x", [nseq * SEQ, DM], F32)
    w_in_d = din("w_in", [1024, 3584], F32)
    w_gate_d = din("w_gate", [1024, 2048], F32)
    w_attn_d = din("w_attn", [1024, 1024], F32)
    w_four_d = din("w_four", [512, 1024], F32)
    w_out_d = din("w_out", [1024, 1024], F32)
    w_up_d = din("w_up", [1024, 4096], F32)
    w_down_d = din("w_down", [4096, 1024], F32)
    vecs_d = din("vecs", [128, 36], F32)
    lams_d = din("lams", [128, 4, 64], F32)
    cmat_d = din("cmat", [128, 8, 128], BF16)
    rope_d = din("rope", [128, 2, SEQ], F32)
    dftc_d = din("dftc", [SEQ, SEQ], BF16)
    dfts_d = din("dfts", [SEQ, SEQ], BF16)
    nyq_d = din("nyq", [128, 16], BF16)
    y_d = nc.dram_tensor("y", [nseq * SEQ, DM], F32, kind="ExternalOutput").ap()
    if debug:
        dbg_hT = nc.dram_tensor("dbg_hT", [128, 8, SEQ], BF16, kind="ExternalOutput").ap()
        dbg_oT = nc.dram_tensor("dbg_oT", [128, 8, SEQ], BF16, kind="ExternalOutput").ap()
        dbg_frT = nc.dram_tensor("dbg_frT", [128, 4, SEQ], BF16, kind="ExternalOutput").ap()

    wb_vf = dscr("wb_vf", [3, 128, 4096], BF16)
    wb_qk = dscr("wb_qk", [8, 128, 2048], BF16)
    wb_mix = dscr("wb_mix", [8, 128, 3584], BF16)
    wb_out = dscr("wb_out", [2, 128, 4096], BF16)
    wb_up = dscr("wb_up", [8, 128, 4096], BF16)
    wb_down = dscr("wb_down", [8, 128, 4096], BF16)

    with ExitStack() as st:
        P = Prog(nc, st)

        def sb(name, shape, dt):
            return st.enter_context(nc.sbuf_tensor("sb_" + name, list(shape), dt))

        pall = st.enter_context(nc.psum_tensor("pall", [128, 8, 512], F32))
        cmat = sb("cmat", [128, 8, 128], BF16)
        dm = sb("dm", [128, 64], BF16)
        rope = sb("rope", [128, 2, SEQ], F32)
        vecs = sb("vecs", [128, 36], F32)
        lams = sb("lams", [128, 4, 64], F32)
        sm = sb("sm", [128, 32], F32)
        nyq = sb("nyq", [128, 16], BF16)
        ycn = sb("ycn", [128, 4], BF16)
        hT = sb("hT", [128, 8, SEQ], BF16)
        oT = sb("oT", [128, 8, SEQ], BF16)
        frT = sb("frT", [128, 4, SEQ], BF16)
        NWST = 3
        wst = [sb("wst%d" % i, [128, 4096], BF16) for i in range(NWST)]
        ov = sb("ov", [128, OVN], BF16)

        ident, blk, rotm, ones, onesm, ccm, nscm, esel = [cmat[:, i, :] for i in range(8)]
        cosT, sinT = rope[:, 0, :], rope[:, 1, :]
        gm, gmlp, bg = vecs[:, 0:8], vecs[:, 8:16], vecs[:, 16:32]
        gq, gk, gsub = vecs[:, 32:33], vecs[:, 33:34], vecs[:, 34:35]
        epsc = sm[:, 0:1]
        neglam = sm[:, 1:2]
        gsub8 = sm[:, 2:3]

        def ovb(lo, n):
            return ov[:, lo:lo + n]

        def ovf(lo, n):
            return ov[:, lo:lo + n].bitcast(F32)

        def bank(i):
            return pall[:, i, :]

        def bres(i):
            return ("bank", i)

        def MM(out, lhsT, rhs, start, stop, reads, writes, tp=None):
            if tp is None:
                P.op("pe", lambda e: e.matmul(out, lhsT=lhsT, rhs=rhs, start=start, stop=stop), reads, writes)
            else:
                P.op("pe", lambda e: e.matmul(out, lhsT=lhsT, rhs=rhs, start=start, stop=stop, tile_position=tp), reads, writes)

        def TR(out, in_, reads, writes):
            P.op("pe", lambda e: e.transpose(out, in_, ident), reads + ["cmat"], writes)

        def ACT(out, in_, func, reads, writes, bias=None, scale=None, accum_out=None):
            kw = {}
            if bias is not None:
                kw["bias"] = bias
            if scale is not None:
                kw["scale"] = scale
            if accum_out is not None:
                kw["accum_out"] = accum_out
            P.op("act", lambda e: e.activation(out=out, in_=in_, func=func, **kw), reads, writes)

        def TT(out, in0, in1, op, reads, writes, eng="dve"):
            P.op(eng, lambda e: e.tensor_tensor(out=out, in0=in0, in1=in1, op=op), reads, writes)

        def STT(out, in0, scalar, in1, op0, op1, reads, writes, eng="dve"):
            P.op(eng, lambda e: e.scalar_tensor_tensor(out=out, in0=in0, scalar=scalar, in1=in1, op0=op0, op1=op1), reads, writes)

        def TS1(out, in0, scalar1, op0, reads, writes, eng="dve"):
            P.op(eng, lambda e: e.tensor_scalar(out=out, in0=in0, scalar1=scalar1, scalar2=None, op0=op0), reads, writes)

        def CP(out, in_, reads, writes, eng="dve"):
            if eng == "act":
                ACT(out, in_, AF.Copy, reads, writes)
            else:
                P.op(eng, lambda e: e.tensor_copy(out=out, in_=in_), reads, writes)

        def RCP(out, in_, reads, writes):
            P.op("dve", lambda e: e.reciprocal(out=out, in_=in_), reads, writes)

        def DMA(q, out, in_, slot, reads, writes):
            P.dma(q, lambda e: e.dma_start(out=out, in_=in_), slot, reads=reads, writes=writes)

        WRES = [[("wst", b, 0)] for b in range(NWST)]
        wcnt = [0]

        def wload(seq, src, ncols, reads, view=None):
            b = wcnt[0] % NWST
            wcnt[0] += 1
            dst = wst[b][:, 0:ncols]
            if view is not None:
                dst = dst.rearrange(view[0], **view[1])
            DMA("sp", dst, src, ("wst", b, seq), reads, [WRES[b][0]])
            return b

        DMA("sp", cmat[:], cmat_d, "c0", [], ["cmat"])
        DMA("sp", vecs[:], vecs_d, "c1", [], ["vecs"])
        DMA("sp", lams[:], lams_d, "c2", [], ["lams"])
        DMA("sp", rope[:], rope_d, "c3", [], ["rope"])
        DMA("sp", nyq[:], nyq_d, "c4", [], ["nyq"])

        w_in_r = w_in_d.rearrange("(k p) n -> p k n", p=128)
        w_gate_r = w_gate_d.rearrange("(k p) n -> p k n", p=128)
        w_attn_r = w_attn_d.rearrange("(k p) n -> p k n", p=128)
        w_four_r = w_four_d.rearrange("(k p) n -> p k n", p=128)
        w_out_r = w_out_d.rearrange("(k p) n -> p k n", p=128)
        w_up_r = w_up_d.rearrange("(k p) n -> p k n", p=128)
        w_down_r = w_down_d.rearrange("(f p) n -> p f n", p=128)

        R_VFA = [("wvf", 0), ("wvf", 2)]
        R_VF1 = [("wvf", 1)]
        R_QK = [[("wqk", h, 0), ("wqk", h, 1)] for h in range(8)]
        R_MIX = [("wmix", j, i) for j in range(8) for i in range(4)]
        R_OUT = [("wout", ch) for ch in range(2)]
        R_UP = [("wup", fg) for fg in range(8)]
        R_DOWN = [("wdown", fg) for fg in range(8)]

        def conv_vf(c, slot):
            c0 = (2048, 2560, 3072)[c]
            DMA("pool", wb_vf[c].rearrange("p (k n) -> p k n", k=8), w_in_r[:, :, c0:c0 + 512], slot, [], [("wvf", c)])

        def conv_qk(h):
            d = wb_qk[h].rearrange("p (w k n) -> p w k n", w=2, k=8)
            DMA("pool", d[:, 0], w_in_r[:, :, h * 128:(h + 1) * 128], ("cv", "qk", h), [], [("wqk", h, 0)])
            DMA("pool", d[:, 1], w_in_r[:, :, 1024 + h * 128:1024 + (h + 1) * 128], ("cv", "qk", h), [], [("wqk", h, 1)])

        conv_vf(0, ("cv", "vfA"))
        conv_vf(2, ("cv", "vfA"))
        conv_qk(0)
        conv_vf(1, ("cv", "vf1"))
        for h in range(1, 8):
            conv_qk(h)
        late_list = []

        def late(dst, src, slot, res):
            late_list.append(lambda: DMA("pool", dst, src, slot, [], [res]))

        for j in range(8):
            d = wb_mix[j]
            sl_ = ("cv", "mix")
            late(d[:, 0:1024].rearrange("p (k n) -> p k n", k=8), w_attn_r[:, :, j * 128:(j + 1) * 128], sl_, ("wmix", j, 0))
            late(d[:, 1024:1536].rearrange("p (k n) -> p k n", k=4), w_four_r[:, :, j * 128:(j + 1) * 128], sl_, ("wmix", j, 1))
            late(d[:, 1536:2560].rearrange("p (k n) -> p k n", k=8), w_gate_r[:, :, j * 128:(j + 1) * 128], sl_, ("wmix", j, 2))
            late(d[:, 2560:3584].rearrange("p (k n) -> p k n", k=8), w_gate_r[:, :, 1024 + j * 128:1024 + (j + 1) * 128], sl_, ("wmix", j, 3))
        for ch in range(2):
            late(wb_out[ch].rearrange("p (k n) -> p k n", k=8), w_out_r[:, :, ch * 512:(ch + 1) * 512], ("cv", "out"), ("wout", ch))
        for fg in range(8):
            late(wb_up[fg].rearrange("p (k n) -> p k n", k=8), w_up_r[:, :, fg * 512:(fg + 1) * 512], ("cv", "up"), ("wup", fg))
        for fg in range(8):
            late(wb_down[fg].rearrange("p (f n) -> p f n", f=4), w_down_r[:, fg * 4:(fg + 1) * 4, :], ("cv", "down"), ("wdown", fg))

        P.op("dve", lambda e: e.memset(epsc, EPS), [], ["sm_eps"])
        lt = ovf(0, 256)
        TT(lt[:, 0:64], lams[:, 0, :], lams[:, 1, :], ALU.mult, ["lams"], ovc(0, 256))
        TT(lt[:, 64:128], lams[:, 2, :], lams[:, 3, :], ALU.mult, ["lams"], ovc(0, 256))
        P.op("dve", lambda e: e.reduce_sum(out=sm[:, 4:5], in_=lt[:, 0:64], axis=AX.X), ovc(0, 256), ["sm_l0"])
        P.op("dve", lambda e: e.reduce_sum(out=sm[:, 5:6], in_=lt[:, 64:128], axis=AX.X), ovc(0, 256), ["sm_l1"])
        ACT(sm[:, 6:8], sm[:, 4:6], AF.Exp, ["sm_l0", "sm_l1"], ["sm_e"])
        TT(sm[:, 8:9], sm[:, 7:8], sm[:, 6:7], ALU.subtract, ["sm_e"], ["sm_d"])
        P.op("dve", lambda e: e.tensor_scalar_add(out=neglam, in0=sm[:, 8:9], scalar1=-LAMBDA_INIT), ["sm_d"], ["neglam"])
        P.op("dve", lambda e: e.tensor_scalar_mul(out=gsub8, in0=gsub, scalar1=1.0 - LAMBDA_INIT), ["vecs"], ["gsub8"])
        P.op("dve", lambda e: e.tensor_copy(out=dm[0:64, :], in_=cmat[0:64, 7, 0:64]), ["cmat"], ["dm_top"])
        P.op("dve", lambda e: e.tensor_scalar(out=dm[64:128, :], in0=cmat[64:128, 7, 0:64], scalar1=sm[64:128, 1:2], scalar2=None, op0=ALU.mult),
             ["cmat", "neglam"], ["dm_bot"])

        bcnt = [0]

        def nb():
            b = bcnt[0] % 8
            bcnt[0] += 1
            return b

        O_V4, O_F, O_QT, O_KT = 0, 8192, 16384, 18432
        O_XA = (20480, 22528)
        O_HB = (24576, 25600)
        O_PREP = (26624, 30720)
        O_PT = (34816, 35840, 40960)
        O_FIN = 36864
        O_UT, O_YST, O_X1, O_MIX, O_H2T = 0, 0, 16384, 24576, 28672
        O_HB2 = (32768, 33792)
        O_DT = (34816, 36864)
        O_RL = (38912, 39424)
        gm_bc = gm.unsqueeze(2).to_broadcast([128, 8, 128])
        gmlp_bc = gmlp.unsqueeze(2).to_broadcast([128, 8, 128])

        def rms_rstd(ss_col, rs_col, reads, tagw):
            ACT(sm[:, rs_col:rs_col + 1], sm[:, ss_col:ss_col + 1], AF.Ln, reads + ["sm_eps"], [tagw + "ln"], bias=epsc, scale=1.0 / DM)
            ACT(sm[:, rs_col:rs_col + 1], sm[:, rs_col:rs_col + 1], AF.Exp, [tagw + "ln"], [tagw], scale=-0.5)

        def hT_res(tc):
            return [("hT", tc * 4 + i) for i in range(4)]

        for s in range(nseq):
            r0 = s * SEQ
            bV = wload(s, wb_vf[0], 4096, R_VFA)
            bF = wload(s, wb_vf[2], 4096, R_VFA)

            def a0_norm(t, s=s, r0=r0):
                i = t % 2
                xa = ovf(O_XA[i], 2048)
                hb = ovb(O_HB[i], 1024)
                c_xa, c_hb = ovc(O_XA[i], 2048), ovc(O_HB[i], 1024)
                DMA("sp", xa, x_d[r0 + t * 128:r0 + (t + 1) * 128, :], ("xa", i, s), [], c_xa)
                ACT(hb, xa, AF.Square, c_xa, c_hb + [("ssx", i)], accum_out=sm[:, 10 + i:11 + i])
                rms_rstd(10 + i, 12 + i, [("ssx", i)], "rsx%d" % i)
                TS1(hb, xa, sm[:, 12 + i:13 + i], ALU.mult, c_xa + ["rsx%d" % i], c_hb)

            def a0_tr(t):
                i = t % 2
                hb = ovb(O_HB[i], 1024)
                c_hb = ovc(O_HB[i], 1024)
                b = nb()
                pt = bank(b).bitcast(BF16)
                for j in range(8):
                    TR(pt[:, j * 128:(j + 1) * 128], hb[:, j * 128:(j + 1) * 128], c_hb, [bres(b)])
                TT(hT[:, :, t * 128:(t + 1) * 128], pt.rearrange("p (j t) -> p j t", j=8), gm_bc, ALU.mult,
                   [bres(b), "vecs"], [("hT", t)])

            def vproj(t, bw, dst_lo, eng):
                b = nb()
                for k in range(8):
                    MM(bank(b), hT[:, k, t * 128:(t + 1) * 128], wst[bw][:, k * 512:(k + 1) * 512], k == 0, k == 7,
                       [("hT", t)] + WRES[bw], [bres(b)])
                CP(ovb(dst_lo + t * 512, 512), bank(b), [bres(b)], ovc(dst_lo + t * 512, 512), eng=eng)

            a0_norm(0)
            a0_norm(1)
            a0_tr(0)
            for t in range(NT):
                if t + 2 < NT:
                    a0_norm(t + 2)
                if t + 1 < NT:
                    a0_tr(t + 1)
                vproj(t, bV, O_V4, "act")
                vproj(t, bF, O_F, "dve")

            O_YC, O_YS = O_PREP[0], O_PREP[0] + 2048
            O_WS = (O_PREP[0] + 4096, O_PREP[0] + 5120)
            bn = 0
            for g in range(4):
                for t in range(NT):
                    MM(pall[:, bn, g:g + 1], ovb(O_F + t * 512 + g * 128, 128), nyq[:, t:t + 1], t == 0, t == 15,
                       ovc(O_F + t * 512, 512) + ["nyq"], [bres(bn)])
            CP(ycn[:, :], pall[:, bn, 0:4], [bres(bn)], ["ycn"])
            bn2 = 1
            for g in range(4):
                MM(pall[:, bn2, g:g + 1], ccm, ycn[:, g:g + 1], True, True, ["ycn", "cmat"], [bres(bn2)])
            for g in range(4):
                CP(frT[:, g, 1024:1025], pall[:, bn2, g:g + 1], [bres(bn2)], [("frT", g, 2)])
            for tq in range(2):
                for mat, dsrc in ((0, dftc_d), (1, dfts_d)):
                    for half in range(2):
                        src = dsrc[half * 1024:(half + 1) * 1024, tq * 512:(tq + 1) * 512].rearrange("(t p) n -> p t n", p=128)
                        bw = wload(s, src, 4096, [], view=("p (t n) -> p t n", dict(t=8)))
                        for g in range(4):
                            b = mat * 4 + g
                            for tl in range(8):
                                t = half * 8 + tl
                                MM(bank(b), ovb(O_F + t * 512 + g * 128, 128), wst[bw][:, tl * 512:(tl + 1) * 512], t == 0, t == 15,
                                   ovc(O_F + t * 512, 512) + WRES[bw], [bres(b)])
                for g in range(4):
                    CP(ovb(O_YC + g * 512, 512), bank(g), [bres(g)], ovc(O_YC + g * 512, 512), eng="act")
                    CP(ovb(O_YS + g * 512, 512), bank(4 + g), [bres(4 + g)], ovc(O_YS + g * 512, 512), eng="dve")
                for g in range(4):
                    bu, bw_ = 2 * g, 2 * g + 1
                    ws = O_WS[g % 2]
                    MM(bank(bu), ccm, ovb(O_YC + g * 512, 512), True, True, ovc(O_YC + g * 512, 512) + ["cmat"], [bres(bu)])
                    MM(bank(bw_), nscm, ovb(O_YS + g * 512, 512), True, True, ovc(O_YS + g * 512, 512) + ["cmat"], [bres(bw_)])
                    CP(ovf(ws, 1024), bank(bw_), [bres(bw_)], ovc(ws, 1024), eng="act")
                    TT(frT[:, g, tq * 512:(tq + 1) * 512], bank(bu), ovf(ws, 1024), ALU.add, [bres(bu)] + ovc(ws, 1024), [("frT", g, tq)])
                    if tq == 0:
                        TT(frT[:, g, 2047:1536:-1], pall[:, bu, 1:512], ovf(ws, 1024)[:, 1:512], ALU.subtract,
                           [bres(bu)] + ovc(ws, 1024), [("frT", g, 3)])
                    else:
                        TT(frT[:, g, 1536:1024:-1], bank(bu), ovf(ws, 1024), ALU.subtract,
                           [bres(bu)] + ovc(ws, 1024), [("frT", g, 2), ("frT", g, 3)])

            ucnt = [0]
            def head_body(h, prev_tail, s=s):
                if h == 4:
                    bV2 = wload(s, wb_vf[1], 4096, R_VF1)
                    for t in range(NT):
                        vproj(t, bV2, O_V4, "act" if t % 2 == 0 else "dve")
                bq = wload(s, wb_qk[h], 2048, R_QK[h])
                NU = 8
                bz_, bs_, br_ = {}, {}, {}
                PB = O_PREP[0]
                o_sq_ = [PB + 512 * i for i in range(2)]
                o_rs_ = [PB + 1024 + 1024 * i for i in range(3)]
                o_xn_ = [PB + 4096 + 512 * i for i in range(3)]
                o_t1_ = [PB + 5632 + 1024 * i for i in range(2)]

                def u_info(n):
                    which, tc = n // 4, n % 4
                    return which, tc, slice(tc * 512, (tc + 1) * 512)

                def stA(n):
                    which, tc, tsl = u_info(n)
                    bz = nb()
                    bz_[n] = bz
                    for k in range(8):
                        MM(bank(bz), wst[bq][:, which * 1024 + k * 128:which * 1024 + (k + 1) * 128], hT[:, k, tsl], k == 0, k == 7,
                           hT_res(tc) + WRES[bq], [bres(bz)])

                def stB(n):
                    o_sq = o_sq_[n % 2]
                    ACT(ovb(o_sq, 512), bank(bz_[n]), AF.Square, [bres(bz_[n])], ovc(o_sq, 512))

                def stC(n):
                    o_sq = o_sq_[n % 2]
                    bs = nb()
                    bs_[n] = bs
                    MM(bank(bs), blk, ovb(o_sq, 512), True, True, ovc(o_sq, 512) + ["cmat"], [bres(bs)])

                def stD(n):
                    o_rs = o_rs_[n % 3]
                    ACT(ovf(o_rs, 1024), bank(bs_[n]), AF.Ln, [bres(bs_[n]), "sm_eps"], ovc(o_rs, 1024), bias=epsc, scale=1.0)
                    ACT(ovf(o_rs, 1024), ovf(o_rs, 1024), AF.Exp, ovc(o_rs, 1024), ovc(o_rs, 1024), scale=-0.5)

                def stE(n):
                    which, tc, tsl = u_info(n)
                    gvec = gq if which == 0 else gk
                    o_rs, o_xn = o_rs_[n % 3], o_xn_[n % 3]
                    STT(ovb(o_xn, 512), bank(bz_[n]), gvec, ovf(o_rs, 1024), ALU.mult, ALU.mult,
                        [bres(bz_[n]), "vecs"] + ovc(o_rs, 1024), ovc(o_xn, 512))

                def stF(n):
                    o_xn = o_xn_[n % 3]
                    br = nb()
                    br_[n] = br
                    MM(bank(br), rotm, ovb(o_xn, 512), True, True, ovc(o_xn, 512) + ["cmat"], [bres(br)])

                def stG(n):
                    which, tc, tsl = u_info(n)
                    dst_lo = O_QT if which == 0 else O_KT
                    o_rs, o_xn, o_t1 = o_rs_[n % 3], o_xn_[n % 3], o_t1_[n % 2]
                    TT(ovf(o_t1, 1024), ovb(o_xn, 512), cosT[:, tsl], ALU.mult, ovc(o_xn, 512) + ["rope"], ovc(o_t1, 1024), eng="pool")
                    TT(ovf(o_rs, 1024), bank(br_[n]), sinT[:, tsl], ALU.mult, [bres(br_[n]), "rope"], ovc(o_rs, 1024))
                    TT(ovb(dst_lo + tc * 512, 256), ovf(o_t1, 512), ovf(o_rs, 512), ALU.add,
                       ovc(o_t1, 1024) + ovc(o_rs, 1024), ovc(dst_lo + tc * 512, 512))
                    TT(ovb(dst_lo + tc * 512 + 256, 256), ovf(o_t1 + 512, 512), ovf(o_rs + 512, 512), ALU.add,
                       ovc(o_t1, 1024) + ovc(o_rs, 1024), ovc(dst_lo + tc * 512, 512), eng="pool")

                prev_tail = list(prev_tail)
                for tau in range(NU + 2):
                    if tau >= 1 and prev_tail:
                        prev_tail.pop(0)()
                    if tau < NU:
                        stA(tau)
                    if 0 <= tau - 1 < NU:
                        stC(tau - 1)
                    if 0 <= tau - 2 < NU:
                        stF(tau - 2)
                    if 0 <= tau - 1 < NU:
                        stD(tau - 1)
                    if tau < NU:
                        stB(tau)
                    if 0 <= tau - 1 < NU:
                        stE(tau - 1)
                    if 0 <= tau - 2 < NU:
                        stG(tau - 2)

                for _ in range(8):
                    if late_list:
                        late_list.pop(0)()
                hv = (h % 4) * 128
                MS_AT = 11
                FIN_AT = (0, 3, 5, 6)
                items = []
                pend = []
                for qc in range(NTC):
                    for kt in range(16):
                        items.append(("u", qc, kt))
                        for ent in list(pend):
                            if len(items) >= ent[0]:
                                items.append(("ms", ent[1], 0))
                                pend.remove(ent)
                    pend.append((len(items) + MS_AT - 1, qc))
                for ent in pend:
                    items.append(("ms", ent[1], 0))
                o_as, o_bs, o_ss, o_ab, o_bb = O_FIN, O_FIN + 1024, O_FIN + 2048, O_FIN + 3072, O_FIN + 3584
                o_os, o_sq, o_rs = o_as, o_bs, o_ss
                nu = [0]
                deferred = {}

                def defer(at, fn):
                    deferred.setdefault(at, []).append(fn)

                def prod(i):
                    kind, qc, kt = items[i]
                    sl = i % 2
                    if kind == "u":
                        for m in range(2):
                            MM(bank(2 * sl + m), ov[64 * m:64 * (m + 1), O_KT + kt * 128:O_KT + (kt + 1) * 128],
                               ov[64 * m:64 * (m + 1), O_QT + qc * 512:O_QT + (qc + 1) * 512], True, True,
                               ovc(O_KT + kt * 128, 128) + ovc(O_QT + qc * 512, 512), [bres(2 * sl + m)])
                    else:
                        MM(bank(2 * sl), onesm, ovb(o_sq, 512), True, True, ovc(o_sq, 512) + ["cmat"], [bres(2 * sl)])

                def fin_a():
                    RCP(ovf(o_ss, 1024), ovf(o_ss, 1024), ovc(o_ss, 1024), ovc(o_ss, 1024))

                def fin_b():
                    TT(ovb(o_ab, 512), ovf(o_as, 1024), ovf(o_ss, 1024), ALU.mult, ovc(o_as, 1024) + ovc(o_ss, 1024), ovc(o_ab, 512))
                    TT(ovb(o_bb, 512), ovf(o_bs, 1024), ovf(o_ss, 1024), ALU.mult, ovc(o_bs, 1024) + ovc(o_ss, 1024), ovc(o_bb, 512))

                def fin_c():
                    MM(pall[0:64, 7, :], dm[:, :], ovb(o_ab, 512), True, True, ovc(o_ab, 512) + ["dm_top", "dm_bot"], [bres(7)])
                    MM(pall[64:128, 7, :], dm[:, :], ovb(o_bb, 512), True, True, ovc(o_bb, 512) + ["dm_top", "dm_bot"], [bres(7)], tp=(0, 64))

                def fin_d():
                    CP(ovf(o_os, 1024), bank(7), [bres(7)], ovc(o_os, 1024))
                    TT(ovb(o_sq, 512), ovf(o_os, 1024), ovf(o_os, 1024), ALU.mult, ovc(o_os, 1024), ovc(o_sq, 512), eng="pool")

                def cons(i):
                    kind, qc, kt = items[i]
                    sl = i % 2
                    if kind == "ms":
                        bm = 2 * sl
                        ACT(ovf(o_rs, 1024), bank(bm), AF.Ln, [bres(bm), "sm_eps"], ovc(o_rs, 1024), bias=epsc, scale=1.0)
                        ACT(ovf(o_rs, 1024), ovf(o_rs, 1024), AF.Exp, ovc(o_rs, 1024), ovc(o_rs, 1024), scale=-0.5)
                        STT(oT[:, h, qc * 512:(qc + 1) * 512], ovf(o_os, 1024), gsub8, ovf(o_rs, 1024), ALU.mult, ALU.mult,
                            ovc(o_os, 1024) + ovc(o_rs, 1024) + ["gsub8"], [("oT", h, qc)])
                        if i + 2 < n_main:
                            prod(i + 2)
                        return
                    pt_lo = O_PT[nu[0] % 3]
                    nu[0] += 1
                    ACT(ovb(pt_lo, 1024).rearrange("p (a b) -> p a b", a=2), pall[:, 2 * sl:2 * sl + 2, :], AF.Exp,
                        [bres(2 * sl), bres(2 * sl + 1)], ovc(pt_lo, 1024), scale=0.125)
                    for fn in deferred.pop(i, []):
                        fn()
                    if i + 2 < n_main:
                        prod(i + 2)
                    st_, sp_ = (kt == 0), (kt == 15)
                    rv = ovc(O_V4 + kt * 512, 512) + ovc(pt_lo, 1024)
                    for half in range(2):
                        vap = ovb(O_V4 + kt * 512 + hv + 64 * half, 64)
                        MM(pall[0:64, 4 + half, :], vap, ovb(pt_lo, 512), st_, sp_, rv, [bres(4 + half)])
                        MM(pall[64:128, 4 + half, :], vap, ovb(pt_lo + 512, 512), st_, sp_, rv, [bres(4 + half)], tp=(0, 64))
                    MM(pall[0:64, 6, :], ones[:, 0:64], ovb(pt_lo, 512), st_, sp_, ovc(pt_lo, 1024) + ["cmat"], [bres(6)])
                    MM(pall[64:128, 6, :], ones[:, 64:128], ovb(pt_lo + 512, 512), st_, sp_, ovc(pt_lo, 1024) + ["cmat"], [bres(6)], tp=(0, 64))
                    if kt == 15:
                        CP(ovf(o_as, 1024), bank(4), [bres(4)], ovc(o_as, 1024))
                        CP(ovf(o_bs, 1024), bank(5), [bres(5)], ovc(o_bs, 1024))
                        CP(ovf(o_ss, 1024), bank(6), [bres(6)], ovc(o_ss, 1024))
                        nxt = [j for j in range(i + 1, len(items)) if items[j][0] == "u"]
                        for step, fn in enumerate((fin_a, fin_b, fin_c, fin_d)):
                            if FIN_AT[step] < len(nxt):
                                defer(nxt[FIN_AT[step]], fn)
                            else:
                                defer(-1, fn)

                final_ms = items.pop()
                assert final_ms[0] == "ms" and final_ms[1] == NTC - 1
                n_main = len(items)
                prod(0)
                prod(1)
                for i in range(n_main):
                    cons(i)
                while prev_tail:
                    prev_tail.pop(0)()
                tl = list(deferred.pop(-1, []))
                assert not deferred and len(tl) == 4
                tail = [tl[0], tl[1], (lambda c=tl[2], d=tl[3]: (c(), d()))]
                items.append(final_ms)

                def last_ms():
                    prod(len(items) - 1)
                    cons(len(items) - 1)
                tail.append(last_ms)
                return tail


            tail = []
            for h in range(8):
                tail = head_body(h, tail)
            for fn in tail:
                fn()

            if debug and s == 0:
                DMA("sp", dbg_hT, hT[:], "dbg0", [("hT", t) for t in range(NT)], ["dbg0"])
                DMA("sp", dbg_oT, oT[:], "dbg1", [("oT", h, qc) for h in range(8) for qc in range(NTC)], ["dbg1"])
                DMA("sp", dbg_frT, frT[:], "dbg2", [("frT", g, tq) for g in range(4) for tq in range(NTC)], ["dbg2"])
            while late_list:
                late_list.pop(0)()
            x1 = ovf(O_X1, 8192).rearrange("p (t d) -> p t d", t=4)

            def d_loadx(tc, s=s, r0=r0):
                DMA("pool", x1, x_d[r0 + tc * 512:r0 + (tc + 1) * 512, :].rearrange("(t p) d -> p t d", p=128),
                    ("x1", s), [], ovc(O_X1, 8192))

            def d_mix(tc, s=s):
                tsl = slice(tc * 512, (tc + 1) * 512)
                for j in range(8):
                    bw = wload(s, wb_mix[j], 3584, R_MIX)
                    di = j % 2
                    o_s0, o_s1 = O_DT[di], O_DT[di] + 1024
                    bA, bB, bC, bD = nb(), nb(), nb(), nb()
                    for hh in range(8):
                        MM(bank(bA), wst[bw][:, hh * 128:(hh + 1) * 128], oT[:, hh, tsl], hh == 0, hh == 7,
                           [("oT", hh, tc)] + WRES[bw], [bres(bA)])
                    for g in range(4):
                        MM(bank(bB), wst[bw][:, 1024 + g * 128:1024 + (g + 1) * 128], frT[:, g, tsl], g == 0, g == 3,
                           [("frT", g, tc)] + WRES[bw], [bres(bB)])
                    for gi, bG in ((0, bC), (1, bD)):
                        for k in range(8):
                            c0 = 1536 + gi * 1024 + k * 128
                            MM(bank(bG), wst[bw][:, c0:c0 + 128], hT[:, k, tsl], k == 0, k == 7,
                               hT_res(tc) + WRES[bw], [bres(bG)])
                    ACT(ovf(o_s0, 1024), bank(bC), AF.Sigmoid, [bres(bC), "vecs"], ovc(o_s0, 1024), bias=bg[:, j:j + 1], scale=1.0)
                    ACT(ovf(o_s1, 1024), bank(bD), AF.Sigmoid, [bres(bD), "vecs"], ovc(o_s1, 1024), bias=bg[:, 8 + j:9 + j], scale=1.0)
                    TT(ovf(o_s0, 1024), bank(bA), ovf(o_s0, 1024), ALU.mult, [bres(bA)] + ovc(o_s0, 1024), ovc(o_s0, 1024))
                    TT(ovf(o_s1, 1024), bank(bB), ovf(o_s1, 1024), ALU.mult, [bres(bB)] + ovc(o_s1, 1024), ovc(o_s1, 1024))
                    TT(ovb(O_MIX + j * 512, 512), ovf(o_s0, 1024), ovf(o_s1, 1024), ALU.add, ovc(o_s0, 2048), ovc(O_MIX + j * 512, 512))

            def d_norm(tt):
                i = tt % 2
                xo = O_X1 + tt * 2048
                hb2 = ovb(O_HB2[i], 1024)
                c_hb2 = ovc(O_HB2[i], 1024)
                ACT(hb2, ovf(xo, 2048), AF.Square, ovc(xo, 2048), c_hb2 + [("ssy", i)], accum_out=sm[:, 14 + i:15 + i])
                rms_rstd(14 + i, 16 + i, [("ssy", i)], "rsy%d" % i)
                TS1(hb2, ovf(xo, 2048), sm[:, 16 + i:17 + i], ALU.mult, ovc(xo, 2048) + ["rsy%d" % i], c_hb2)

            def d_tr(tt):
                i = tt % 2
                hb2 = ovb(O_HB2[i], 1024)
                c_hb2 = ovc(O_HB2[i], 1024)
                b = nb()
                pt = bank(b).bitcast(BF16)
                for j in range(8):
                    TR(pt[:, j * 128:(j + 1) * 128], hb2[:, j * 128:(j + 1) * 128], c_hb2, [bres(b)])
                h2v = ovb(O_H2T, 4096).rearrange("p (j t) -> p j t", j=8)[:, :, tt * 128:(tt + 1) * 128]
                TT(h2v, pt.rearrange("p (j t) -> p j t", j=8), gmlp_bc, ALU.mult, [bres(b), "vecs"], ovc(O_H2T, 4096))

            def d_out_norm(tc, s=s):
                bwo = [wload(s, wb_out[ch], 4096, R_OUT) for ch in range(2)]
                for tt in range(4):
                    for ch in range(2):
                        bw = bwo[ch]
                        b = nb()
                        for k in range(8):
                            MM(bank(b), ovb(O_MIX + k * 512 + tt * 128, 128), wst[bw][:, k * 512:(k + 1) * 512], k == 0, k == 7,
                               ovc(O_MIX + k * 512, 512) + WRES[bw], [bres(b)])
                        xo = O_X1 + tt * 2048 + ch * 1024
                        TT(ovf(xo, 1024), bank(b), ovf(xo, 1024), ALU.add, [bres(b)] + ovc(xo, 1024), ovc(xo, 1024))
                    d_norm(tt)
                    if tt >= 1:
                        d_tr(tt - 1)

            def d_mlp(tc, s=s, r0=r0):
                for fg in range(8):
                    bw = wload(s, wb_up[fg], 4096, R_UP)
                    for f4 in range(4):
                        f = fg * 4 + f4
                        b = nb()
                        for k in range(8):
                            MM(bank(b), wst[bw][:, k * 512 + f4 * 128:k * 512 + (f4 + 1) * 128], ovb(O_H2T + k * 512, 512), k == 0, k == 7,
                               ovc(O_H2T, 4096) + WRES[bw], [bres(b)])
                        ri = f % 2
                        ACT(ovb(O_RL[ri], 512), bank(b), AF.Relu, [bres(b)], ovc(O_RL[ri], 512))
                        TT(ovb(O_UT + f * 512, 512), ovb(O_RL[ri], 512), ovb(O_RL[ri], 512), ALU.mult, ovc(O_RL[ri], 512), ovc(O_UT + f * 512, 512))
                for fg in range(8):
                    bw = wload(s, wb_down[fg], 4096, R_DOWN)
                    for f4 in range(4):
                        f = fg * 4 + f4
                        for tt in range(4):
                            for ch in range(2):
                                b = tt * 2 + ch
                                MM(bank(b), ovb(O_UT + f * 512 + tt * 128, 128), wst[bw][:, f4 * 1024 + ch * 512:f4 * 1024 + (ch + 1) * 512],
                                   f == 0, f == 31, ovc(O_UT + f * 512, 512) + WRES[bw], [bres(b)])
                bcnt[0] = 0
                for tt in range(4):
                    for ch in range(2):
                        b = tt * 2 + ch
                        xo = O_X1 + tt * 2048 + ch * 1024
                        yo = O_YST + tt * 2048 + ch * 1024
                        TT(ovf(yo, 1024), bank(b), ovf(xo, 1024), ALU.add, [bres(b)] + ovc(xo, 1024), ovc(yo, 1024))
                DMA("pool", y_d[r0 + tc * 512:r0 + (tc + 1) * 512, :].rearrange("(t p) d -> p t d", p=128),
                    ovf(O_YST, 8192).rearrange("p (t d) -> p t d", t=4), ("yst", s), ovc(O_YST, 8192), [("y", s, tc)])

            d_loadx(0)
            d_mix(0)
            for tc in range(NTC):
                d_out_norm(tc)
                if tc + 1 < NTC:
                    d_mix(tc + 1)
                d_tr(3)
                d_mlp(tc)
                if tc + 1 < NTC:
                    d_loadx(tc + 1)

        yres = [("y", s, tc) for s in range(nseq) for tc in range(NTC)]
        P.op("sp", lambda e: None, yres + (["dbg0", "dbg1", "dbg2"] if debug else []), [])
        P.op("pool", lambda e: None, yres, [])
        P.emit()
    return nc


def _consts():
    bf = ml_dtypes.bfloat16
    ident = np.eye(128, dtype=np.float32)
    blk = np.zeros((128, 128), np.float32)
    blk[:64, :64] = 1.0 / 64
    blk[64:, 64:] = 1.0 / 64
    rot = np.zeros((128, 128), np.float32)
    for o in (0, 64):
        for m in range(64):
            if m < 32:
                rot[o + m + 32, o + m] = -1.0
            else:
                rot[o + m - 32, o + m] = 1.0
    ones = np.ones((128, 128), np.float32)
    onesm = np.full((128, 128), 1.0 / 128, np.float32)
    c = np.arange(128)
    ang = 2.0 * np.pi * ((c[:, None] * c[None, :]) % 128) / 128.0
    cc = np.cos(ang) / np.sqrt(128.0)
    nsc = -np.sin(ang) / np.sqrt(128.0)
    esel = np.zeros((128, 128), np.float32)
    for k_ in range(128):
        esel[k_, k_ % 64] = 1.0
    cmat = np.stack([ident, blk, rot, ones, onesm, cc, nsc, esel], axis=1).astype(bf)
    half = 32
    freqs = (np.float32(10000.0) ** (-np.arange(half, dtype=np.float32) * np.float32(2.0) / np.float32(64))).astype(np.float32)
    angs = (np.arange(SEQ, dtype=np.float32)[:, None] * freqs[None, :]).astype(np.float32)
    cos, sin = np.cos(angs).astype(np.float32), np.sin(angs).astype(np.float32)
    idx = np.arange(128) % 32
    rope = np.stack([cos.T[idx], sin.T[idx]], axis=1).astype(np.float32)
    t = np.arange(SEQ, dtype=np.int64)
    a2 = 2.0 * np.pi * ((t[:, None] * t[None, :]) % SEQ).astype(np.float64) / SEQ
    dftc = (np.cos(a2) / np.sqrt(float(SEQ))).astype(bf)
    dfts = (np.sin(a2) / np.sqrt(float(SEQ))).astype(bf)
    tt_ = np.arange(SEQ)
    nyqcol = (np.where(tt_ % 2 == 0, 1.0, -1.0) / np.sqrt(float(SEQ))).astype(np.float32)
    nyq = np.ascontiguousarray(nyqcol.reshape(16, 128).T).astype(bf)
    return np.ascontiguousarray(cmat), np.ascontiguousarray(rope), dftc, dfts, nyq


_CACHE = {}


def _run(xs, weights, nseq, debug=False):
    if "consts" not in _CACHE:
        _CACHE["consts"] = _consts()
    cmat, rope, dftc, dfts, nyq = _CACHE["consts"]
    key = ("nc", nseq, debug)
    if key not in _CACHE:
        _CACHE[key] = build_program(nseq, debug)
    nc = _CACHE[key]
    (g_mix, w_in, g_q, g_k, lq1, lk1, lq2, lk2, g_sub, w_attn, w_four, w_gate, b_gate, w_out, g_mlp, w_up, w_down) = weights
    vecs = np.zeros((128, 36), np.float32)
    vecs[:, 0:8] = g_mix.reshape(8, 128).T
    vecs[:, 8:16] = g_mlp.reshape(8, 128).T
    vecs[:, 16:32] = b_gate.reshape(16, 128).T
    vecs[:, 32] = np.tile(g_q, 2)
    vecs[:, 33] = np.tile(g_k, 2)
    vecs[:, 34] = g_sub
    lams = np.ascontiguousarray(np.broadcast_to(np.stack([lq1, lk1, lq2, lk2])[None], (128, 4, 64))).astype(np.float32)
    shared = {
        "w_in": np.ascontiguousarray(w_in), "w_gate": np.ascontiguousarray(w_gate), "w_attn": np.ascontiguousarray(w_attn),
        "w_four": np.ascontiguousarray(w_four), "w_out": np.ascontiguousarray(w_out), "w_up": np.ascontiguousarray(w_up),
        "w_down": np.ascontiguousarray(w_down), "vecs": vecs, "lams": lams, "cmat": cmat, "rope": rope,
        "dftc": dftc, "dfts": dfts, "nyq": nyq,
    }
    in_maps = [dict(shared, x=np.ascontiguousarray(x)) for x in xs]
    res = run_bass_kernel_spmd(nc, in_maps, core_ids=list(range(len(xs))))
    if debug:
        return res.results
    return [np.asarray(r["y"]) for r in res.results]


def kernel(x_prompt, x_sample, g_mix, w_in, g_q, g_k, lam_q1, lam_k1, lam_q2, lam_k2,
           g_sub, w_attn_br, w_four_br, w_gate, b_gate, w_out, g_mlp, w_up, w_down):
    f = lambda a: np.asarray(a, dtype=np.float32)
    xp, xs_ = f(x_prompt), f(x_sample)
    xall = np.concatenate([xp, xs_], axis=0)
    nb_, ns_ = xp.shape[0], xs_.shape[0]
    assert xall.shape[0] == N_CORES * NSEQ_CORE
    weights = (f(g_mix)[0], f(w_in)[0], f(g_q)[0], f(g_k)[0], f(lam_q1)[0], f(lam_k1)[0], f(lam_q2)[0], f(lam_k2)[0],
               f(g_sub)[0], f(w_attn_br)[0], f(w_four_br)[0], f(w_gate)[0], f(b_gate)[0], f(w_out)[0], f(g_mlp)[0],
               f(w_up)[0], f(w_down)[0])
    xs = [xall[c * NSEQ_CORE:(c + 1) * NSEQ_CORE].reshape(NSEQ_CORE * SEQ, DM) for c in range(N_CORES)]
    ys = _run(xs, weights, NSEQ_CORE)
    yall = np.stack(ys, axis=0).reshape(N_CORES * NSEQ_CORE, SEQ, DM)
    return (np.ascontiguousarray(yall[:nb_]), np.ascontiguousarray(yall[nb_:nb_ + ns_]))
```

```python
import math
from contextlib import ExitStack

import numpy as np
import ml_dtypes

import concourse.bass as bass
import concourse.mybir as mybir
from concourse.bass_utils import run_bass_kernel_spmd

F32 = mybir.dt.float32
BF16 = mybir.dt.bfloat16
AF = mybir.ActivationFunctionType
ALU = mybir.AluOpType
AX = mybir.AxisListType

N_CORES = 8
SEQ = 2048
DM = 1024
NSEQ_CORE = 3
NT = SEQ // 128
NTC = SEQ // 512
EPS = 1e-6
LAMBDA_INIT = 0.8 - 0.6 * math.exp(-0.3 * 0)

ENGS = ("pe", "act", "dve", "pool", "sp")
EMBED_WAIT = True


class Instr:
    __slots__ = ("eng", "fn", "deps", "is_dma", "sem", "target", "signal")

    def __init__(self, eng, fn, is_dma=False):
        self.eng = eng
        self.fn = fn
        self.deps = []
        self.is_dma = is_dma
        self.sem = None
        self.target = 0
        self.signal = False


class Prog:
    def __init__(self, nc, stack):
        self.nc = nc
        self.stack = stack
        self.streams = {e: [] for e in ENGS}
        self.writer = {}
        self.readers = {}
        self.eng_sem = {}
        for e in ("pe", "act", "dve", "pool"):
            self.eng_sem[e] = stack.enter_context(nc.semaphore("ms_" + e))
        self.dma_sems = {}

    def _track(self, ins, reads, writes):
        deps = ins.deps
        for r in reads:
            w = self.writer.get(r)
            if w is not None:
                deps.append((w, 0))
            self.readers.setdefault(r, []).append(ins)
        for r in writes:
            w = self.writer.get(r)
            if w is not None and w is not ins:
                deps.append((w, 1))
            for rd in self.readers.get(r, ()):
                if rd is not ins:
                    deps.append((rd, 2))
            self.writer[r] = ins
            self.readers[r] = []

    def op(self, eng, fn, reads=(), writes=()):
        ins = Instr(eng, fn)
        self._track(ins, reads, writes)
        self.streams[eng].append(ins)
        return ins

    def dma(self, eng, fn, slot, reads=(), writes=()):
        ins = Instr(eng, fn, is_dma=True)
        if slot not in self.dma_sems:
            sem = self.stack.enter_context(self.nc.semaphore("dq%d" % len(self.dma_sems)))
            self.dma_sems[slot] = [sem, 0]
        ent = self.dma_sems[slot]
        ent[1] += 16
        ins.sem, ins.target = ent[0], ent[1]
        self._track(ins, reads, writes)
        self.streams[eng].append(ins)
        return ins

    @staticmethod
    def _needed(ins, dep, kind):
        if dep.is_dma:
            return True
        if dep.eng == ins.eng and not ins.is_dma:
            return ins.eng != "pe"
        return True

    def emit(self):
        nc = self.nc
        for e in ENGS:
            for ins in self.streams[e]:
                for dep, kind in ins.deps:
                    if not dep.is_dma and self._needed(ins, dep, kind):
                        dep.signal = True
        for e in ENGS:
            cnt = 0
            for ins in self.streams[e]:
                if not ins.is_dma and ins.signal:
                    cnt += 1
                    ins.sem, ins.target = self.eng_sem[e], cnt
        handles = {"pe": "tensor", "act": "scalar", "dve": "vector", "pool": "gpsimd", "sp": "sync"}
        with nc.Block() as block:
            for e in ENGS:
                stream = self.streams[e]

                def body(engine, stream=stream):
                    waited = {}
                    for ins in stream:
                        need = {}
                        for dep, kind in ins.deps:
                            if not self._needed(ins, dep, kind):
                                continue
                            key = id(dep.sem)
                            if dep.target > need.get(key, (None, 0))[1]:
                                need[key] = (dep.sem, dep.target)
                        todo = [(sem, tgt, key) for key, (sem, tgt) in need.items() if waited.get(key, 0) < tgt]
                        emb = None
                        if todo and EMBED_WAIT:
                            emb = todo.pop()
                        for sem, tgt, key in todo:
                            engine.wait_ge(sem, tgt)
                            waited[key] = tgt
                        r = ins.fn(engine)
                        if emb is not None:
                            if r is None:
                                engine.wait_ge(emb[0], emb[1])
                            else:
                                r._wait_ge(emb[0], emb[1])
                            waited[emb[2]] = emb[1]
                        if ins.is_dma:
                            r.then_inc(ins.sem, 16)
                        elif ins.signal:
                            r.then_inc(ins.sem, 1)

                getattr(block, handles[e])(body)


OVN = 41984
CELL = 512


def ovc(lo, n):
    return [("ov", c) for c in range(lo // CELL, (lo + n + CELL - 1) // CELL)]


def build_program(nseq, debug=False):
    nc = bass.Bass("TRN2", target_bir_lowering=False)

    def din(name, shape, dt):
        return nc.dram_tensor(name, list(shape), dt, kind="ExternalInput").ap()

    def dscr(name, shape, dt):
        return nc.dram_tensor(name, list(shape), dt, kind="Internal").ap()

    x_d = din("x", [nseq * SEQ, DM], F32)
    w_in_d = din("w_in", [1024, 3584], F32)
    w_gate_d = din("w_gate", [1024, 2048], F32)
    w_attn_d = din("w_attn", [1024, 1024], F32)
    w_four_d = din("w_four", [512, 1024], F32)
    w_out_d = din("w_out", [1024, 1024], F32)
    w_up_d = din("w_up", [1024, 4096], F32)
    w_down_d = din("w_down", [4096, 1024], F32)
    vecs_d = din("vecs", [128, 36], F32)
    lams_d = din("lams", [128, 4, 64], F32)
    cmat_d = din("cmat", [128, 8, 128], BF16)
    rope_d = din("rope", [128, 2, SEQ], F32)
    dftc_d = din("dftc", [SEQ, SEQ], BF16)
    dfts_d = din("dfts", [SEQ, SEQ], BF16)
    nyq_d = din("nyq", [128, 16], BF16)
    y_d = nc.dram_tensor("y", [nseq * SEQ, DM], F32, kind="ExternalOutput").ap()
    if debug:
        dbg_hT = nc.dram_tensor("dbg_hT", [128, 8, SEQ], BF16, kind="ExternalOutput").ap()
        dbg_oT = nc.dram_tensor("dbg_oT", [128, 8, SEQ], BF16, kind="ExternalOutput").ap()
        dbg_frT = nc.dram_tensor("dbg_frT", [128, 4, SEQ], BF16, kind="ExternalOutput").ap()

    wb_vf = dscr("wb_vf", [3, 128, 4096], BF16)
    wb_qk = dscr("wb_qk", [8, 128, 2048], BF16)
    wb_mix = dscr("wb_mix", [8, 128, 3584], BF16)
    wb_out = dscr("wb_out", [2, 128, 4096], BF16)
    wb_up = dscr("wb_up", [8, 128, 4096], BF16)
    wb_down = dscr("wb_down", [8, 128, 4096], BF16)

    with ExitStack() as st:
        P = Prog(nc, st)

        def sb(name, shape, dt):
            return st.enter_context(nc.sbuf_tensor("sb_" + name, list(shape), dt))

        pall = st.enter_context(nc.psum_tensor("pall", [128, 8, 512], F32))
        cmat = sb("cmat", [128, 8, 128], BF16)
        dm = sb("dm", [128, 64], BF16)
        rope = sb("rope", [128, 2, SEQ], F32)
        vecs = sb("vecs", [128, 36], F32)
        lams = sb("lams", [128, 4, 64], F32)
        sm = sb("sm", [128, 32], F32)
        nyq = sb("nyq", [128, 16], BF16)
        ycn = sb("ycn", [128, 4], BF16)
        hT = sb("hT", [128, 8, SEQ], BF16)
        oT = sb("oT", [128, 8, SEQ], BF16)
        frT = sb("frT", [128, 4, SEQ], BF16)
        NWST = 3
        wst = [sb("wst%d" % i, [128, 4096], BF16) for i in range(NWST)]
        ov = sb("ov", [128, OVN], BF16)

        ident, blk, rotm, ones, onesm, ccm, nscm, esel = [cmat[:, i, :] for i in range(8)]
        cosT, sinT = rope[:, 0, :], rope[:, 1, :]
        gm, gmlp, bg = vecs[:, 0:8], vecs[:, 8:16], vecs[:, 16:32]
        gq, gk, gsub = vecs[:, 32:33], vecs[:, 33:34], vecs[:, 34:35]
        epsc = sm[:, 0:1]
        neglam = sm[:, 1:2]
        gsub8 = sm[:, 2:3]

        def ovb(lo, n):
            return ov[:, lo:lo + n]

        def ovf(lo, n):
            return ov[:, lo:lo + n].bitcast(F32)

        def bank(i):
            return pall[:, i, :]

        def bres(i):
            return ("bank", i)

        def MM(out, lhsT, rhs, start, stop, reads, writes, tp=None):
            if tp is None:
                P.op("pe", lambda e: e.matmul(out, lhsT=lhsT, rhs=rhs, start=start, stop=stop), reads, writes)
            else:
                P.op("pe", lambda e: e.matmul(out, lhsT=lhsT, rhs=rhs, start=start, stop=stop, tile_position=tp), reads, writes)

        def TR(out, in_, reads, writes):
            P.op("pe", lambda e: e.transpose(out, in_, ident), reads + ["cmat"], writes)

        def ACT(out, in_, func, reads, writes, bias=None, scale=None, accum_out=None):
            kw = {}
            if bias is not None:
                kw["bias"] = bias
            if scale is not None:
                kw["scale"] = scale
            if accum_out is not None:
                kw["accum_out"] = accum_out
            P.op("act", lambda e: e.activation(out=out, in_=in_, func=func, **kw), reads, writes)

        def TT(out, in0, in1, op, reads, writes, eng="dve"):
            P.op(eng, lambda e: e.tensor_tensor(out=out, in0=in0, in1=in1, op=op), reads, writes)

        def STT(out, in0, scalar, in1, op0, op1, reads, writes, eng="dve"):
            P.op(eng, lambda e: e.scalar_tensor_tensor(out=out, in0=in0, scalar=scalar, in1=in1, op0=op0, op1=op1), reads, writes)

        def TS1(out, in0, scalar1, op0, reads, writes, eng="dve"):
            P.op(eng, lambda e: e.tensor_scalar(out=out, in0=in0, scalar1=scalar1, scalar2=None, op0=op0), reads, writes)

        def CP(out, in_, reads, writes, eng="dve"):
            if eng == "act":
                ACT(out, in_, AF.Copy, reads, writes)
            else:
                P.op(eng, lambda e: e.tensor_copy(out=out, in_=in_), reads, writes)

        def RCP(out, in_, reads, writes):
            P.op("dve", lambda e: e.reciprocal(out=out, in_=in_), reads, writes)

        def DMA(q, out, in_, slot, reads, writes):
            P.dma(q, lambda e: e.dma_start(out=out, in_=in_), slot, reads=reads, writes=writes)

        WRES = [[("wst", b, 0)] for b in range(NWST)]
        wcnt = [0]

        def wload(seq, src, ncols, reads, view=None):
            b = wcnt[0] % NWST
            wcnt[0] += 1
            dst = wst[b][:, 0:ncols]
            if view is not None:
                dst = dst.rearrange(view[0], **view[1])
            DMA("sp", dst, src, ("wst", b, seq), reads, [WRES[b][0]])
            return b

        DMA("sp", cmat[:], cmat_d, "c0", [], ["cmat"])
        DMA("sp", vecs[:], vecs_d, "c1", [], ["vecs"])
        DMA("sp", lams[:], lams_d, "c2", [], ["lams"])
        DMA("sp", rope[:], rope_d, "c3", [], ["rope"])
        DMA("sp", nyq[:], nyq_d, "c4", [], ["nyq"])

        w_in_r = w_in_d.rearrange("(k p) n -> p k n", p=128)
        w_gate_r = w_gate_d.rearrange("(k p) n -> p k n", p=128)
        w_attn_r = w_attn_d.rearrange("(k p) n -> p k n", p=128)
        w_four_r = w_four_d.rearrange("(k p) n -> p k n", p=128)
        w_out_r = w_out_d.rearrange("(k p) n -> p k n", p=128)
        w_up_r = w_up_d.rearrange("(k p) n -> p k n", p=128)
        w_down_r = w_down_d.rearrange("(f p) n -> p f n", p=128)

        R_VFA = [("wvf", 0), ("wvf", 2)]
        R_VF1 = [("wvf", 1)]
        R_QK = [[("wqk", h, 0), ("wqk", h, 1)] for h in range(8)]
        R_MIX = [("wmix", j, i) for j in range(8) for i in range(4)]
        R_OUT = [("wout", ch) for ch in range(2)]
        R_UP = [("wup", fg) for fg in range(8)]
        R_DOWN = [("wdown", fg) for fg in range(8)]

        def conv_vf(c, slot):
            c0 = (2048, 2560, 3072)[c]
            DMA("pool", wb_vf[c].rearrange("p (k n) -> p k n", k=8), w_in_r[:, :, c0:c0 + 512], slot, [], [("wvf", c)])

        def conv_qk(h):
            d = wb_qk[h].rearrange("p (w k n) -> p w k n", w=2, k=8)
            DMA("pool", d[:, 0], w_in_r[:, :, h * 128:(h + 1) * 128], ("cv", "qk", h), [], [("wqk", h, 0)])
            DMA("pool", d[:, 1], w_in_r[:, :, 1024 + h * 128:1024 + (h + 1) * 128], ("cv", "qk", h), [], [("wqk", h, 1)])

        conv_vf(0, ("cv", "vfA"))
        conv_vf(2, ("cv", "vfA"))
        conv_qk(0)
        conv_vf(1, ("cv", "vf1"))
        for h in range(1, 8):
            conv_qk(h)
        late_list = []

        def late(dst, src, slot, res):
            late_list.append(lambda: DMA("pool", dst, src, slot, [], [res]))

        for j in range(8):
            d = wb_mix[j]
            sl_ = ("cv", "mix")
            late(d[:, 0:1024].rearrange("p (k n) -> p k n", k=8), w_attn_r[:, :, j * 128:(j + 1) * 128], sl_, ("wmix", j, 0))
            late(d[:, 1024:1536].rearrange("p (k n) -> p k n", k=4), w_four_r[:, :, j * 128:(j + 1) * 128], sl_, ("wmix", j, 1))
            late(d[:, 1536:2560].rearrange("p (k n) -> p k n", k=8), w_gate_r[:, :, j * 128:(j + 1) * 128], sl_, ("wmix", j, 2))
            late(d[:, 2560:3584].rearrange("p (k n) -> p k n", k=8), w_gate_r[:, :, 1024 + j * 128:1024 + (j + 1) * 128], sl_, ("wmix", j, 3))
        for ch in range(2):
            late(wb_out[ch].rearrange("p (k n) -> p k n", k=8), w_out_r[:, :, ch * 512:(ch + 1) * 512], ("cv", "out"), ("wout", ch))
        for fg in range(8):
            late(wb_up[fg].rearrange("p (k n) -> p k n", k=8), w_up_r[:, :, fg * 512:(fg + 1) * 512], ("cv", "up"), ("wup", fg))
        for fg in range(8):
            late(wb_down[fg].rearrange("p (f n) -> p f n", f=4), w_down_r[:, fg * 4:(fg + 1) * 4, :], ("cv", "down"), ("wdown", fg))

        P.op("dve", lambda e: e.memset(epsc, EPS), [], ["sm_eps"])
        lt = ovf(0, 256)
        TT(lt[:, 0:64], lams[:, 0, :], lams[:, 1, :], ALU.mult, ["lams"], ovc(0, 256))
        TT(lt[:, 64:128], lams[:, 2, :], lams[:, 3, :], ALU.mult, ["lams"], ovc(0, 256))
        P.op("dve", lambda e: e.reduce_sum(out=sm[:, 4:5], in_=lt[:, 0:64], axis=AX.X), ovc(0, 256), ["sm_l0"])
        P.op("dve", lambda e: e.reduce_sum(out=sm[:, 5:6], in_=lt[:, 64:128], axis=AX.X), ovc(0, 256), ["sm_l1"])
        ACT(sm[:, 6:8], sm[:, 4:6], AF.Exp, ["sm_l0", "sm_l1"], ["sm_e"])
        TT(sm[:, 8:9], sm[:, 7:8], sm[:, 6:7], ALU.subtract, ["sm_e"], ["sm_d"])
        P.op("dve", lambda e: e.tensor_scalar_add(out=neglam, in0=sm[:, 8:9], scalar1=-LAMBDA_INIT), ["sm_d"], ["neglam"])
        P.op("dve", lambda e: e.tensor_scalar_mul(out=gsub8, in0=gsub, scalar1=1.0 - LAMBDA_INIT), ["vecs"], ["gsub8"])
        P.op("dve", lambda e: e.tensor_copy(out=dm[0:64, :], in_=cmat[0:64, 7, 0:64]), ["cmat"], ["dm_top"])
        P.op("dve", lambda e: e.tensor_scalar(out=dm[64:128, :], in0=cmat[64:128, 7, 0:64], scalar1=sm[64:128, 1:2], scalar2=None, op0=ALU.mult),
             ["cmat", "neglam"], ["dm_bot"])

        bcnt = [0]

        def nb():
            b = bcnt[0] % 8
            bcnt[0] += 1
            return b

        O_V4, O_F, O_QT, O_KT = 0, 8192, 16384, 18432
        O_XA = (20480, 22528)
        O_HB = (24576, 25600)
        O_PREP = (26624, 30720)
        O_PT = (34816, 35840, 40960)
        O_FIN = 36864
        O_UT, O_YST, O_X1, O_MIX, O_H2T = 0, 0, 16384, 24576, 28672
        O_HB2 = (32768, 33792)
        O_DT = (34816, 36864)
        O_RL = (38912, 39424)
        gm_bc = gm.unsqueeze(2).to_broadcast([128, 8, 128])
        gmlp_bc = gmlp.unsqueeze(2).to_broadcast([128, 8, 128])

        def rms_rstd(ss_col, rs_col, reads, tagw):
            ACT(sm[:, rs_col:rs_col + 1], sm[:, ss_col:ss_col + 1], AF.Ln, reads + ["sm_eps"], [tagw + "ln"], bias=epsc, scale=1.0 / DM)
            ACT(sm[:, rs_col:rs_col + 1], sm[:, rs_col:rs_col + 1], AF.Exp, [tagw + "ln"], [tagw], scale=-0.5)

        def hT_res(tc):
            return [("hT", tc * 4 + i) for i in range(4)]

        for s in range(nseq):
            r0 = s * SEQ
            bV = wload(s, wb_vf[0], 4096, R_VFA)
            bF = wload(s, wb_vf[2], 4096, R_VFA)

            def a0_norm(t, s=s, r0=r0):
                i = t % 2
                xa = ovf(O_XA[i], 2048)
                hb = ovb(O_HB[i], 1024)
                c_xa, c_hb = ovc(O_XA[i], 2048), ovc(O_HB[i], 1024)
                DMA("sp", xa, x_d[r0 + t * 128:r0 + (t + 1) * 128, :], ("xa", i, s), [], c_xa)
                ACT(hb, xa, AF.Square, c_xa, c_hb + [("ssx", i)], accum_out=sm[:, 10 + i:11 + i])
                rms_rstd(10 + i, 12 + i, [("ssx", i)], "rsx%d" % i)
                TS1(hb, xa, sm[:, 12 + i:13 + i], ALU.mult, c_xa + ["rsx%d" % i], c_hb)

            def a0_tr(t):
                i = t % 2
                hb = ovb(O_HB[i], 1024)
                c_hb = ovc(O_HB[i], 1024)
                b = nb()
                pt = bank(b).bitcast(BF16)
                for j in range(8):
                    TR(pt[:, j * 128:(j + 1) * 128], hb[:, j * 128:(j + 1) * 128], c_hb, [bres(b)])
                TT(hT[:, :, t * 128:(t + 1) * 128], pt.rearrange("p (j t) -> p j t", j=8), gm_bc, ALU.mult,
                   [bres(b), "vecs"], [("hT", t)])

            def vproj(t, bw, dst_lo, eng):
                b = nb()
                for k in range(8):
                    MM(bank(b), hT[:, k, t * 128:(t + 1) * 128], wst[bw][:, k * 512:(k + 1) * 512], k == 0, k == 7,
                       [("hT", t)] + WRES[bw], [bres(b)])
                CP(ovb(dst_lo + t * 512, 512), bank(b), [bres(b)], ovc(dst_lo + t * 512, 512), eng=eng)

            a0_norm(0)
            a0_norm(1)
            a0_tr(0)
            for t in range(NT):
                if t + 2 < NT:
                    a0_norm(t + 2)
                if t + 1 < NT:
                    a0_tr(t + 1)
                vproj(t, bV, O_V4, "act")
                vproj(t, bF, O_F, "dve")

            O_YC, O_YS = O_PREP[0], O_PREP[0] + 2048
            O_WS = (O_PREP[0] + 4096, O_PREP[0] + 5120)
            bn = 0
            for g in range(4):
                for t in range(NT):
                    MM(pall[:, bn, g:g + 1], ovb(O_F + t * 512 + g * 128, 128), nyq[:, t:t + 1], t == 0, t == 15,
                       ovc(O_F + t * 512, 512) + ["nyq"], [bres(bn)])
            CP(ycn[:, :], pall[:, bn, 0:4], [bres(bn)], ["ycn"])
            bn2 = 1
            for g in range(4):
                MM(pall[:, bn2, g:g + 1], ccm, ycn[:, g:g + 1], True, True, ["ycn", "cmat"], [bres(bn2)])
            for g in range(4):
                CP(frT[:, g, 1024:1025], pall[:, bn2, g:g + 1], [bres(bn2)], [("frT", g, 2)])
            for tq in range(2):
                for mat, dsrc in ((0, dftc_d), (1, dfts_d)):
                    for half in range(2):
                        src = dsrc[half * 1024:(half + 1) * 1024, tq * 512:(tq + 1) * 512].rearrange("(t p) n -> p t n", p=128)
                        bw = wload(s, src, 4096, [], view=("p (t n) -> p t n", dict(t=8)))
                        for g in range(4):
                            b = mat * 4 + g
                            for tl in range(8):
                                t = half * 8 + tl
                                MM(bank(b), ovb(O_F + t * 512 + g * 128, 128), wst[bw][:, tl * 512:(tl + 1) * 512], t == 0, t == 15,
                                   ovc(O_F + t * 512, 512) + WRES[bw], [bres(b)])
                for g in range(4):
                    CP(ovb(O_YC + g * 512, 512), bank(g), [bres(g)], ovc(O_YC + g * 512, 512), eng="act")
                    CP(ovb(O_YS + g * 512, 512), bank(4 + g), [bres(4 + g)], ovc(O_YS + g * 512, 512), eng="dve")
                for g in range(4):
                    bu, bw_ = 2 * g, 2 * g + 1
                    ws = O_WS[g % 2]
                    MM(bank(bu), ccm, ovb(O_YC + g * 512, 512), True, True, ovc(O_YC + g * 512, 512) + ["cmat"], [bres(bu)])
                    MM(bank(bw_), nscm, ovb(O_YS + g * 512, 512), True, True, ovc(O_YS + g * 512, 512) + ["cmat"], [bres(bw_)])
                    CP(ovf(ws, 1024), bank(bw_), [bres(bw_)], ovc(ws, 1024), eng="act")
                    TT(frT[:, g, tq * 512:(tq + 1) * 512], bank(bu), ovf(ws, 1024), ALU.add, [bres(bu)] + ovc(ws, 1024), [("frT", g, tq)])
                    if tq == 0:
                        TT(frT[:, g, 2047:1536:-1], pall[:, bu, 1:512], ovf(ws, 1024)[:, 1:512], ALU.subtract,
                           [bres(bu)] + ovc(ws, 1024), [("frT", g, 3)])
                    else:
                        TT(frT[:, g, 1536:1024:-1], bank(bu), ovf(ws, 1024), ALU.subtract,
                           [bres(bu)] + ovc(ws, 1024), [("frT", g, 2), ("frT", g, 3)])

            ucnt = [0]
            def head_body(h, prev_tail, s=s):
                if h == 4:
                    bV2 = wload(s, wb_vf[1], 4096, R_VF1)
                    for t in range(NT):
                        vproj(t, bV2, O_V4, "act" if t % 2 == 0 else "dve")
                bq = wload(s, wb_qk[h], 2048, R_QK[h])
                NU = 8
                bz_, bs_, br_ = {}, {}, {}
                PB = O_PREP[0]
                o_sq_ = [PB + 512 * i for i in range(2)]
                o_rs_ = [PB + 1024 + 1024 * i for i in range(3)]
                o_xn_ = [PB + 4096 + 512 * i for i in range(3)]
                o_t1_ = [PB + 5632 + 1024 * i for i in range(2)]

                def u_info(n):
                    which, tc = n // 4, n % 4
                    return which, tc, slice(tc * 512, (tc + 1) * 512)

                def stA(n):
                    which, tc, tsl = u_info(n)
                    bz = nb()
                    bz_[n] = bz
                    for k in range(8):
                        MM(bank(bz), wst[bq][:, which * 1024 + k * 128:which * 1024 + (k + 1) * 128], hT[:, k, tsl], k == 0, k == 7,
                           hT_res(tc) + WRES[bq], [bres(bz)])

                def stB(n):
                    o_sq = o_sq_[n % 2]
                    ACT(ovb(o_sq, 512), bank(bz_[n]), AF.Square, [bres(bz_[n])], ovc(o_sq, 512))

                def stC(n):
                    o_sq = o_sq_[n % 2]
                    bs = nb()
                    bs_[n] = bs
                    MM(bank(bs), blk, ovb(o_sq, 512), True, True, ovc(o_sq, 512) + ["cmat"], [bres(bs)])

                def stD(n):
                    o_rs = o_rs_[n % 3]
                    ACT(ovf(o_rs, 1024), bank(bs_[n]), AF.Ln, [bres(bs_[n]), "sm_eps"], ovc(o_rs, 1024), bias=epsc, scale=1.0)
                    ACT(ovf(o_rs, 1024), ovf(o_rs, 1024), AF.Exp, ovc(o_rs, 1024), ovc(o_rs, 1024), scale=-0.5)

                def stE(n):
                    which, tc, tsl = u_info(n)
                    gvec = gq if which == 0 else gk
                    o_rs, o_xn = o_rs_[n % 3], o_xn_[n % 3]
                    STT(ovb(o_xn, 512), bank(bz_[n]), gvec, ovf(o_rs, 1024), ALU.mult, ALU.mult,
                        [bres(bz_[n]), "vecs"] + ovc(o_rs, 1024), ovc(o_xn, 512))

                def stF(n):
                    o_xn = o_xn_[n % 3]
                    br = nb()
                    br_[n] = br
                    MM(bank(br), rotm, ovb(o_xn, 512), True, True, ovc(o_xn, 512) + ["cmat"], [bres(br)])

                def stG(n):
                    which, tc, tsl = u_info(n)
                    dst_lo = O_QT if which == 0 else O_KT
                    o_rs, o_xn, o_t1 = o_rs_[n % 3], o_xn_[n % 3], o_t1_[n % 2]
                    TT(ovf(o_t1, 1024), ovb(o_xn, 512), cosT[:, tsl], ALU.mult, ovc(o_xn, 512) + ["rope"], ovc(o_t1, 1024), eng="pool")
                    TT(ovf(o_rs, 1024), bank(br_[n]), sinT[:, tsl], ALU.mult, [bres(br_[n]), "rope"], ovc(o_rs, 1024))
                    TT(ovb(dst_lo + tc * 512, 256), ovf(o_t1, 512), ovf(o_rs, 512), ALU.add,
                       ovc(o_t1, 1024) + ovc(o_rs, 1024), ovc(dst_lo + tc * 512, 512))
                    TT(ovb(dst_lo + tc * 512 + 256, 256), ovf(o_t1 + 512, 512), ovf(o_rs + 512, 512), ALU.add,
                       ovc(o_t1, 1024) + ovc(o_rs, 1024), ovc(dst_lo + tc * 512, 512), eng="pool")

                prev_tail = list(prev_tail)
                for tau in range(NU + 2):
                    if tau >= 1 and prev_tail:
                        prev_tail.pop(0)()
                    if tau < NU:
                        stA(tau)
                    if 0 <= tau - 1 < NU:
                        stC(tau - 1)
                    if 0 <= tau - 2 < NU:
                        stF(tau - 2)
                    if 0 <= tau - 1 < NU:
                        stD(tau - 1)
                    if tau < NU:
                        stB(tau)
                    if 0 <= tau - 1 < NU:
                        stE(tau - 1)
                    if 0 <= tau - 2 < NU:
                        stG(tau - 2)

                for _ in range(8):
                    if late_list:
                        late_list.pop(0)()
                hv = (h % 4) * 128
                MS_AT = 11
                FIN_AT = (0, 3, 5, 6)
                items = []
                pend = []
                for qc in range(NTC):
                    for kt in range(16):
                        items.append(("u", qc, kt))
                        for ent in list(pend):
                            if len(items) >= ent[0]:
                                items.append(("ms", ent[1], 0))
                                pend.remove(ent)
                    pend.append((len(items) + MS_AT - 1, qc))
                for ent in pend:
                    items.append(("ms", ent[1], 0))
                o_as, o_bs, o_ss, o_ab, o_bb = O_FIN, O_FIN + 1024, O_FIN + 2048, O_FIN + 3072, O_FIN + 3584
                o_os, o_sq, o_rs = o_as, o_bs, o_ss
                nu = [0]
                deferred = {}

                def defer(at, fn):
                    deferred.setdefault(at, []).append(fn)

                def prod(i):
                    kind, qc, kt = items[i]
                    sl = i % 2
                    if kind == "u":
                        for m in range(2):
                            MM(bank(2 * sl + m), ov[64 * m:64 * (m + 1), O_KT + kt * 128:O_KT + (kt + 1) * 128],
                               ov[64 * m:64 * (m + 1), O_QT + qc * 512:O_QT + (qc + 1) * 512], True, True,
                               ovc(O_KT + kt * 128, 128) + ovc(O_QT + qc * 512, 512), [bres(2 * sl + m)])
                    else:
                        MM(bank(2 * sl), onesm, ovb(o_sq, 512), True, True, ovc(o_sq, 512) + ["cmat"], [bres(2 * sl)])

                def fin_a():
                    RCP(ovf(o_ss, 1024), ovf(o_ss, 1024), ovc(o_ss, 1024), ovc(o_ss, 1024))

                def fin_b():
                    TT(ovb(o_ab, 512), ovf(o_as, 1024), ovf(o_ss, 1024), ALU.mult, ovc(o_as, 1024) + ovc(o_ss, 1024), ovc(o_ab, 512))
                    TT(ovb(o_bb, 512), ovf(o_bs, 1024), ovf(o_ss, 1024), ALU.mult, ovc(o_bs, 1024) + ovc(o_ss, 1024), ovc(o_bb, 512))

                def fin_c():
                    MM(pall[0:64, 7, :], dm[:, :], ovb(o_ab, 512), True, True, ovc(o_ab, 512) + ["dm_top", "dm_bot"], [bres(7)])
                    MM(pall[64:128, 7, :], dm[:, :], ovb(o_bb, 512), True, True, ovc(o_bb, 512) + ["dm_top", "dm_bot"], [bres(7)], tp=(0, 64))

                def fin_d():
                    CP(ovf(o_os, 1024), bank(7), [bres(7)], ovc(o_os, 1024))
                    TT(ovb(o_sq, 512), ovf(o_os, 1024), ovf(o_os, 1024), ALU.mult, ovc(o_os, 1024), ovc(o_sq, 512), eng="pool")

                def cons(i):
                    kind, qc, kt = items[i]
                    sl = i % 2
                    if kind == "ms":
                        bm = 2 * sl
                        ACT(ovf(o_rs, 1024), bank(bm), AF.Ln, [bres(bm), "sm_eps"], ovc(o_rs, 1024), bias=epsc, scale=1.0)
                        ACT(ovf(o_rs, 1024), ovf(o_rs, 1024), AF.Exp, ovc(o_rs, 1024), ovc(o_rs, 1024), scale=-0.5)
                        STT(oT[:, h, qc * 512:(qc + 1) * 512], ovf(o_os, 1024), gsub8, ovf(o_rs, 1024), ALU.mult, ALU.mult,
                            ovc(o_os, 1024) + ovc(o_rs, 1024) + ["gsub8"], [("oT", h, qc)])
                        if i + 2 < n_main:
                            prod(i + 2)
                        return
                    pt_lo = O_PT[nu[0] % 3]
                    nu[0] += 1
                    ACT(ovb(pt_lo, 1024).rearrange("p (a b) -> p a b", a=2), pall[:, 2 * sl:2 * sl + 2, :], AF.Exp,
                        [bres(2 * sl), bres(2 * sl + 1)], ovc(pt_lo, 1024), scale=0.125)
                    for fn in deferred.pop(i, []):
                        fn()
                    if i + 2 < n_main:
                        prod(i + 2)
                    st_, sp_ = (kt == 0), (kt == 15)
                    rv = ovc(O_V4 + kt * 512, 512) + ovc(pt_lo, 1024)
                    for half in range(2):
                        vap = ovb(O_V4 + kt * 512 + hv + 64 * half, 64)
                        MM(pall[0:64, 4 + half, :], vap, ovb(pt_lo, 512), st_, sp_, rv, [bres(4 + half)])
                        MM(pall[64:128, 4 + half, :], vap, ovb(pt_lo + 512, 512), st_, sp_, rv, [bres(4 + half)], tp=(0, 64))
                    MM(pall[0:64, 6, :], ones[:, 0:64], ovb(pt_lo, 512), st_, sp_, ovc(pt_lo, 1024) + ["cmat"], [bres(6)])
                    MM(pall[64:128, 6, :], ones[:, 64:128], ovb(pt_lo + 512, 512), st_, sp_, ovc(pt_lo, 1024) + ["cmat"], [bres(6)], tp=(0, 64))
                    if kt == 15:
                        CP(ovf(o_as, 1024), bank(4), [bres(4)], ovc(o_as, 1024))
                        CP(ovf(o_bs, 1024), bank(5), [bres(5)], ovc(o_bs, 1024))
                        CP(ovf(o_ss, 1024), bank(6), [bres(6)], ovc(o_ss, 1024))
                        nxt = [j for j in range(i + 1, len(items)) if items[j][0] == "u"]
                        for step, fn in enumerate((fin_a, fin_b, fin_c, fin_d)):
                            if FIN_AT[step] < len(nxt):
                                defer(nxt[FIN_AT[step]], fn)
                            else:
                                defer(-1, fn)

                final_ms = items.pop()
                assert final_ms[0] == "ms" and final_ms[1] == NTC - 1
                n_main = len(items)
                prod(0)
                prod(1)
                for i in range(n_main):
                    cons(i)
                while prev_tail:
                    prev_tail.pop(0)()
                tl = list(deferred.pop(-1, []))
                assert not deferred and len(tl) == 4
                tail = [tl[0], tl[1], (lambda c=tl[2], d=tl[3]: (c(), d()))]
                items.append(final_ms)

                def last_ms():
                    prod(len(items) - 1)
                    cons(len(items) - 1)
                tail.append(last_ms)
                return tail


            tail = []
            for h in range(8):
                tail = head_body(h, tail)
            for fn in tail:
                fn()

            if debug and s == 0:
                DMA("sp", dbg_hT, hT[:], "dbg0", [("hT", t) for t in range(NT)], ["dbg0"])
                DMA("sp", dbg_oT, oT[:], "dbg1", [("oT", h, qc) for h in range(8) for qc in range(NTC)], ["dbg1"])
                DMA("sp", dbg_frT, frT[:], "dbg2", [("frT", g, tq) for g in range(4) for tq in range(NTC)], ["dbg2"])
            while late_list:
                late_list.pop(0)()
            NHOIST = 2

            def mix_step(tc, j, s=s):
                tsl = slice(tc * 512, (tc + 1) * 512)
                bw = wload(s, wb_mix[j], 3584, R_MIX)
                di = j % 2
                o_s0, o_s1 = O_DT[di], O_DT[di] + 1024
                bA, bB, bC, bD = nb(), nb(), nb(), nb()
                for hh in range(8):
                    MM(bank(bA), wst[bw][:, hh * 128:(hh + 1) * 128], oT[:, hh, tsl], hh == 0, hh == 7,
                       [("oT", hh, tc)] + WRES[bw], [bres(bA)])
                for g in range(4):
                    MM(bank(bB), wst[bw][:, 1024 + g * 128:1024 + (g + 1) * 128], frT[:, g, tsl], g == 0, g == 3,
                       [("frT", g, tc)] + WRES[bw], [bres(bB)])
                for gi, bG in ((0, bC), (1, bD)):
                    for k in range(8):
                        c0 = 1536 + gi * 1024 + k * 128
                        MM(bank(bG), wst[bw][:, c0:c0 + 128], hT[:, k, tsl], k == 0, k == 7,
                           hT_res(tc) + WRES[bw], [bres(bG)])
                ACT(ovf(o_s0, 1024), bank(bC), AF.Sigmoid, [bres(bC), "vecs"], ovc(o_s0, 1024), bias=bg[:, j:j + 1], scale=1.0)
                ACT(ovf(o_s1, 1024), bank(bD), AF.Sigmoid, [bres(bD), "vecs"], ovc(o_s1, 1024), bias=bg[:, 8 + j:9 + j], scale=1.0)
                TT(ovf(o_s0, 1024), bank(bA), ovf(o_s0, 1024), ALU.mult, [bres(bA)] + ovc(o_s0, 1024), ovc(o_s0, 1024))
                TT(ovf(o_s1, 1024), bank(bB), ovf(o_s1, 1024), ALU.mult, [bres(bB)] + ovc(o_s1, 1024), ovc(o_s1, 1024))
                TT(ovb(O_MIX + j * 512, 512), ovf(o_s0, 1024), ovf(o_s1, 1024), ALU.add, ovc(o_s0, 2048), ovc(O_MIX + j * 512, 512))

            for tc in range(NTC):
                x1 = ovf(O_X1, 8192).rearrange("p (t d) -> p t d", t=4)
                j0 = 0 if tc == 0 else NHOIST
                for j in range(j0, 8):
                    if j == j0:
                        DMA("pool", x1, x_d[r0 + tc * 512:r0 + (tc + 1) * 512, :].rearrange("(t p) d -> p t d", p=128),
                            ("x1", s), [], ovc(O_X1, 8192))
                    mix_step(tc, j)
                bwo = [wload(s, wb_out[ch], 4096, R_OUT) for ch in range(2)]

                def d_norm(tt):
                    i = tt % 2
                    xo = O_X1 + tt * 2048
                    hb2 = ovb(O_HB2[i], 1024)
                    c_hb2 = ovc(O_HB2[i], 1024)
                    ACT(hb2, ovf(xo, 2048), AF.Square, ovc(xo, 2048), c_hb2 + [("ssy", i)], accum_out=sm[:, 14 + i:15 + i])
                    rms_rstd(14 + i, 16 + i, [("ssy", i)], "rsy%d" % i)
                    TS1(hb2, ovf(xo, 2048), sm[:, 16 + i:17 + i], ALU.mult, ovc(xo, 2048) + ["rsy%d" % i], c_hb2)

                def d_tr(tt):
                    i = tt % 2
                    hb2 = ovb(O_HB2[i], 1024)
                    c_hb2 = ovc(O_HB2[i], 1024)
                    b = nb()
                    pt = bank(b).bitcast(BF16)
                    for j in range(8):
                        TR(pt[:, j * 128:(j + 1) * 128], hb2[:, j * 128:(j + 1) * 128], c_hb2, [bres(b)])
                    h2v = ovb(O_H2T, 4096).rearrange("p (j t) -> p j t", j=8)[:, :, tt * 128:(tt + 1) * 128]
                    TT(h2v, pt.rearrange("p (j t) -> p j t", j=8), gmlp_bc, ALU.mult, [bres(b), "vecs"], ovc(O_H2T, 4096))

                for tt in range(4):
                    for ch in range(2):
                        bw = bwo[ch]
                        b = nb()
                        for k in range(8):
                            MM(bank(b), ovb(O_MIX + k * 512 + tt * 128, 128), wst[bw][:, k * 512:(k + 1) * 512], k == 0, k == 7,
                               ovc(O_MIX + k * 512, 512) + WRES[bw], [bres(b)])
                        xo = O_X1 + tt * 2048 + ch * 1024
                        TT(ovf(xo, 1024), bank(b), ovf(xo, 1024), ALU.add, [bres(b)] + ovc(xo, 1024), ovc(xo, 1024))
                    d_norm(tt)
                    if tt >= 1:
                        d_tr(tt - 1)
                if tc + 1 < NTC:
                    for j in range(NHOIST):
                        mix_step(tc + 1, j)
                d_tr(3)
                for fg in range(8):
                    bw = wload(s, wb_up[fg], 4096, R_UP)
                    for f4 in range(4):
                        f = fg * 4 + f4
                        b = nb()
                        for k in range(8):
                            MM(bank(b), wst[bw][:, k * 512 + f4 * 128:k * 512 + (f4 + 1) * 128], ovb(O_H2T + k * 512, 512), k == 0, k == 7,
                               ovc(O_H2T, 4096) + WRES[bw], [bres(b)])
                        ri = f % 2
                        ACT(ovb(O_RL[ri], 512), bank(b), AF.Relu, [bres(b)], ovc(O_RL[ri], 512))
                        TT(ovb(O_UT + f * 512, 512), ovb(O_RL[ri], 512), ovb(O_RL[ri], 512), ALU.mult, ovc(O_RL[ri], 512), ovc(O_UT + f * 512, 512))
                for fg in range(8):
                    bw = wload(s, wb_down[fg], 4096, R_DOWN)
                    for f4 in range(4):
                        f = fg * 4 + f4
                        for tt in range(4):
                            for ch in range(2):
                                b = tt * 2 + ch
                                MM(bank(b), ovb(O_UT + f * 512 + tt * 128, 128), wst[bw][:, f4 * 1024 + ch * 512:f4 * 1024 + (ch + 1) * 512],
                                   f == 0, f == 31, ovc(O_UT + f * 512, 512) + WRES[bw], [bres(b)])
                bcnt[0] = 0
                for tt in range(4):
                    for ch in range(2):
                        b = tt * 2 + ch
                        xo = O_X1 + tt * 2048 + ch * 1024
                        yo = O_YST + tt * 2048 + ch * 1024
                        TT(ovf(yo, 1024), bank(b), ovf(xo, 1024), ALU.add, [bres(b)] + ovc(xo, 1024), ovc(yo, 1024))
                DMA("pool", y_d[r0 + tc * 512:r0 + (tc + 1) * 512, :].rearrange("(t p) d -> p t d", p=128),
                    ovf(O_YST, 8192).rearrange("p (t d) -> p t d", t=4), ("yst", s), ovc(O_YST, 8192), [("y", s, tc)])

        yres = [("y", s, tc) for s in range(nseq) for tc in range(NTC)]
        P.op("sp", lambda e: None, yres + (["dbg0", "dbg1", "dbg2"] if debug else []), [])
        P.op("pool", lambda e: None, yres, [])
        P.emit()
    return nc


def _consts():
    bf = ml_dtypes.bfloat16
    ident = np.eye(128, dtype=np.float32)
    blk = np.zeros((128, 128), np.float32)
    blk[:64, :64] = 1.0 / 64
    blk[64:, 64:] = 1.0 / 64
    rot = np.zeros((128, 128), np.float32)
    for o in (0, 64):
        for m in range(64):
            if m < 32:
                rot[o + m + 32, o + m] = -1.0
            else:
                rot[o + m - 32, o + m] = 1.0
    ones = np.ones((128, 128), np.float32)
    onesm = np.full((128, 128), 1.0 / 128, np.float32)
    c = np.arange(128)
    ang = 2.0 * np.pi * ((c[:, None] * c[None, :]) % 128) / 128.0
    cc = np.cos(ang) / np.sqrt(128.0)
    nsc = -np.sin(ang) / np.sqrt(128.0)
    esel = np.zeros((128, 128), np.float32)
    for k_ in range(128):
        esel[k_, k_ % 64] = 1.0
    cmat = np.stack([ident, blk, rot, ones, onesm, cc, nsc, esel], axis=1).astype(bf)
    half = 32
    freqs = (np.float32(10000.0) ** (-np.arange(half, dtype=np.float32) * np.float32(2.0) / np.float32(64))).astype(np.float32)
    angs = (np.arange(SEQ, dtype=np.float32)[:, None] * freqs[None, :]).astype(np.float32)
    cos, sin = np.cos(angs).astype(np.float32), np.sin(angs).astype(np.float32)
    idx = np.arange(128) % 32
    rope = np.stack([cos.T[idx], sin.T[idx]], axis=1).astype(np.float32)
    t = np.arange(SEQ, dtype=np.int64)
    a2 = 2.0 * np.pi * ((t[:, None] * t[None, :]) % SEQ).astype(np.float64) / SEQ
    dftc = (np.cos(a2) / np.sqrt(float(SEQ))).astype(bf)
    dfts = (np.sin(a2) / np.sqrt(float(SEQ))).astype(bf)
    tt_ = np.arange(SEQ)
    nyqcol = (np.where(tt_ % 2 == 0, 1.0, -1.0) / np.sqrt(float(SEQ))).astype(np.float32)
    nyq = np.ascontiguousarray(nyqcol.reshape(16, 128).T).astype(bf)
    return np.ascontiguousarray(cmat), np.ascontiguousarray(rope), dftc, dfts, nyq


_CACHE = {}


def _run(xs, weights, nseq, debug=False):
    if "consts" not in _CACHE:
        _CACHE["consts"] = _consts()
    cmat, rope, dftc, dfts, nyq = _CACHE["consts"]
    key = ("nc", nseq, debug)
    if key not in _CACHE:
        _CACHE[key] = build_program(nseq, debug)
    nc = _CACHE[key]
    (g_mix, w_in, g_q, g_k, lq1, lk1, lq2, lk2, g_sub, w_attn, w_four, w_gate, b_gate, w_out, g_mlp, w_up, w_down) = weights
    vecs = np.zeros((128, 36), np.float32)
    vecs[:, 0:8] = g_mix.reshape(8, 128).T
    vecs[:, 8:16] = g_mlp.reshape(8, 128).T
    vecs[:, 16:32] = b_gate.reshape(16, 128).T
    vecs[:, 32] = np.tile(g_q, 2)
    vecs[:, 33] = np.tile(g_k, 2)
    vecs[:, 34] = g_sub
    lams = np.ascontiguousarray(np.broadcast_to(np.stack([lq1, lk1, lq2, lk2])[None], (128, 4, 64))).astype(np.float32)
    shared = {
        "w_in": np.ascontiguousarray(w_in), "w_gate": np.ascontiguousarray(w_gate), "w_attn": np.ascontiguousarray(w_attn),
        "w_four": np.ascontiguousarray(w_four), "w_out": np.ascontiguousarray(w_out), "w_up": np.ascontiguousarray(w_up),
        "w_down": np.ascontiguousarray(w_down), "vecs": vecs, "lams": lams, "cmat": cmat, "rope": rope,
        "dftc": dftc, "dfts": dfts, "nyq": nyq,
    }
    in_maps = [dict(shared, x=np.ascontiguousarray(x)) for x in xs]
    res = run_bass_kernel_spmd(nc, in_maps, core_ids=list(range(len(xs))))
    if debug:
        return res.results
    return [np.asarray(r["y"]) for r in res.results]


def kernel(x_prompt, x_sample, g_mix, w_in, g_q, g_k, lam_q1, lam_k1, lam_q2, lam_k2,
           g_sub, w_attn_br, w_four_br, w_gate, b_gate, w_out, g_mlp, w_up, w_down):
    f = lambda a: np.asarray(a, dtype=np.float32)
    xp, xs_ = f(x_prompt), f(x_sample)
    xall = np.concatenate([xp, xs_], axis=0)
    nb_, ns_ = xp.shape[0], xs_.shape[0]
    assert xall.shape[0] == N_CORES * NSEQ_CORE
    weights = (f(g_mix)[0], f(w_in)[0], f(g_q)[0], f(g_k)[0], f(lam_q1)[0], f(lam_k1)[0], f(lam_q2)[0], f(lam_k2)[0],
               f(g_sub)[0], f(w_attn_br)[0], f(w_four_br)[0], f(w_gate)[0], f(b_gate)[0], f(w_out)[0], f(g_mlp)[0],
               f(w_up)[0], f(w_down)[0])
    xs = [xall[c * NSEQ_CORE:(c + 1) * NSEQ_CORE].reshape(NSEQ_CORE * SEQ, DM) for c in range(N_CORES)]
    ys = _run(xs, weights, NSEQ_CORE)
    yall = np.stack(ys, axis=0).reshape(N_CORES * NSEQ_CORE, SEQ, DM)
    return (np.ascontiguousarray(yall[:nb_]), np.ascontiguousarray(yall[nb_:nb_ + ns_]))
```
